# Optimizing a Trainium2 kernel written in Bass

```python
import jax
import jax.numpy as jnp
from jax import lax
import numpy as np


D_MODEL = 1024
BATCH = 16
SEQ = 256
DEPTH = 2
DEC_BATCH = 4
DEC_SEQ = 2048
PAST_LEN = 512

GRID_W = 64
H_A = 8
KV_A = 2
HD_A = 64
H_B = 8
NOPE_B = 64
ROPE_B = 32
VD_B = 64
KV_RANK = 256
D_RNN = 1024
RNN_BLOCKS = 8
RNN_BS = D_RNN // RNN_BLOCKS
RNN_CONV = 4
RNN_PAD = (2, 1)
RG_C = 8.0
D_FF = 2816
FFN_CONV = 3
FFN_PAD = (1, 1)
N_BRANCH = 3
Q_BLOCK = 128
ROPE_THETA = 10000.0
EPS = 1e-6
QA_W = H_A * HD_A
KA_W = KV_A * HD_A
QB_W = H_B * (NOPE_B + ROPE_B)
IN_SECTIONS = (QA_W, KA_W, KA_W, QB_W, KV_RANK, ROPE_B, D_RNN, D_RNN, N_BRANCH * D_MODEL)
IN_SPLITS = tuple(sum(IN_SECTIONS[:i + 1]) for i in range(len(IN_SECTIONS) - 1))
IN_WIDTH = sum(IN_SECTIONS)

kernel_name = 'hybrid_flow_gqa_mla_rglru_convffn_step'


def rmsnorm(x, g):
    xf = x.astype(jnp.float32)
    y = xf * lax.rsqrt(jnp.mean(xf * xf, axis=-1, keepdims=True) + EPS) * g.astype(jnp.float32)
    return y.astype(x.dtype)


def axial_angles(rows, dim):
    row = jnp.repeat(jnp.arange(rows, dtype=jnp.float32), GRID_W)
    col = jnp.tile(jnp.arange(GRID_W, dtype=jnp.float32), rows)
    n = dim // 4
    inv = ROPE_THETA ** (-jnp.arange(n, dtype=jnp.float32) / n)
    ar = row[:, None] * inv
    ac = col[:, None] * inv
    return (jnp.cos(ar), jnp.sin(ar), jnp.cos(ac), jnp.sin(ac))


def rotate(x, cos, sin):
    x1, x2 = jnp.split(x, 2, axis=-1)
    c = cos[:, None, :]
    s = sin[:, None, :]
    return jnp.concatenate([x1 * c - x2 * s, x1 * s + x2 * c], axis=-1)


def axial_rope(x, ang):
    cr, sr, cc, sc = ang
    xr, xc = jnp.split(x.astype(jnp.float32), 2, axis=-1)
    return jnp.concatenate([rotate(xr, cr, sr), rotate(xc, cc, sc)], axis=-1).astype(x.dtype)


def block_attend(q, k, v, scale):
    B, T, KV, G, DH = q.shape
    DV = v.shape[-1]
    nb = T // Q_BLOCK
    qb = jnp.moveaxis(q.reshape(B, nb, Q_BLOCK, KV, G, DH), 1, 0)

    def one(qblk):
        s = jnp.einsum('bqkgd,bskd->bkgqs', qblk, k, preferred_element_type=jnp.float32) * scale
        p = jax.nn.softmax(s, axis=-1).astype(v.dtype)
        return jnp.einsum('bkgqs,bskd->bqkgd', p, v)

    o = lax.map(one, qb)
    return jnp.moveaxis(o, 0, 1).reshape(B, T, KV, G, DV)


def dwconv(x, w, b, pad):
    y = lax.conv_general_dilated(x, w[:, None, :].astype(x.dtype), window_strides=(1,), padding=(pad,),
                                 dimension_numbers=('NWC', 'WIO', 'NWC'), feature_group_count=x.shape[-1])
    return y + b.astype(x.dtype)


def rglru(u, w_r, b_r, w_i, b_i, lam, h0, reverse):
    B, T, _ = u.shape
    uf = u.astype(jnp.float32)
    ub = uf.reshape(B, T, RNN_BLOCKS, RNN_BS)
    r = jax.nn.sigmoid(jnp.einsum('btnc,ncd->btnd', ub, w_r.astype(jnp.float32)).reshape(B, T, D_RNN) + b_r.astype(jnp.float32))
    i = jax.nn.sigmoid(jnp.einsum('btnc,ncd->btnd', ub, w_i.astype(jnp.float32)).reshape(B, T, D_RNN) + b_i.astype(jnp.float32))
    log_a = RG_C * r * jax.nn.log_sigmoid(lam.astype(jnp.float32))
    a = jnp.exp(log_a)
    bx = jnp.sqrt(-jnp.expm1(2.0 * log_a)) * (i * uf)

    def combine(e1, e2):
        a1, b1 = e1
        a2, b2 = e2
        return a1 * a2, a2 * b1 + b2

    A, Hs = lax.associative_scan(combine, (a, bx), axis=1, reverse=reverse)
    return Hs + A * h0.astype(jnp.float32)[:, None, :]


def mixer(h, lp, angs, cache):
    B, T, _ = h.shape
    qa, ka, va, qb, ckv, krope, xr, yr, gl = jnp.split(h @ lp['w_in'], IN_SPLITS, axis=-1)
    qa = rmsnorm(qa.reshape(B, T, H_A, HD_A), lp['g_qa'])
    ka = rmsnorm(ka.reshape(B, T, KV_A, HD_A), lp['g_ka'])
    va = va.reshape(B, T, KV_A, HD_A)
    qb = qb.reshape(B, T, H_B, NOPE_B + ROPE_B)
    ckv = rmsnorm(ckv, lp['g_ckv'])
    kr = krope[:, :, None, :]
    if cache is None:
        ka_all, va_all, ckv_all, kr_all = ka, va, ckv, kr
        h0f = jnp.zeros((B, D_RNN), jnp.float32)
        h0b = h0f
    else:
        ang_a, ang_b = angs
        qa = axial_rope(qa, ang_a)
        ka = axial_rope(ka, ang_a)
        qb = jnp.concatenate([qb[..., :NOPE_B], axial_rope(qb[..., NOPE_B:], ang_b)], axis=-1)
        kr = axial_rope(kr, ang_b)
        ck, cv, cc, ckr, h0f, h0b = cache
        ka_all = jnp.concatenate([ck.astype(ka.dtype), ka], axis=1)
        va_all = jnp.concatenate([cv.astype(va.dtype), va], axis=1)
        ckv_all = jnp.concatenate([cc.astype(ckv.dtype), ckv], axis=1)
        kr_all = jnp.concatenate([ckr[:, :, None, :].astype(kr.dtype), kr], axis=1)
    S = ka_all.shape[1]
    o_a = block_attend(qa.reshape(B, T, KV_A, H_A // KV_A, HD_A), ka_all, va_all, HD_A ** -0.5).reshape(B, T, QA_W)
    kb = jnp.concatenate([(ckv_all @ lp['w_uk']).reshape(B, S, H_B, NOPE_B),
                          jnp.broadcast_to(kr_all, (B, S, H_B, ROPE_B))], axis=-1)
    vb = (ckv_all @ lp['w_uv']).reshape(B, S, H_B, VD_B)
    o_b = block_attend(qb[:, :, :, None, :], kb, vb, (NOPE_B + ROPE_B) ** -0.5).reshape(B, T, H_B * VD_B)
    u = dwconv(xr, lp['conv_rnn_w'], lp['conv_rnn_b'], RNN_PAD)
    hf = rglru(u, lp['w_rg'][0], lp['b_rg'][0], lp['w_ig'][0], lp['b_ig'][0], lp['lam'][0], h0f, False)
    hb = rglru(u, lp['w_rg'][1], lp['b_rg'][1], lp['w_ig'][1], lp['b_ig'][1], lp['lam'][1], h0b, True)
    o_c = jax.nn.gelu(yr) * (hf + hb).astype(yr.dtype)
    g = jax.nn.sigmoid(gl.reshape(B, T, N_BRANCH, D_MODEL))
    merged = g[:, :, 0] * (o_a @ lp['w_oa']) + g[:, :, 1] * (o_b @ lp['w_ob']) + g[:, :, 2] * (o_c @ lp['w_oc'])
    out = merged @ lp['w_out']
    if cache is None:
        return out, (ka, va, ckv, krope, hf[:, -1].astype(h.dtype), hb[:, 0].astype(h.dtype))
    return out, None


def conv_ffn(h, lp):
    up = dwconv(h @ lp['w_up'], lp['conv_ffn_w'], lp['conv_ffn_b'], FFN_PAD)
    val, gat = jnp.split(up, 2, axis=-1)
    return (jax.nn.gelu(gat) * val) @ lp['w_down']


def trunk_layer(x, mod, lp, angs, cache):
    sh1, sc1, gt1, sh2, sc2, gt2 = jnp.split(mod.astype(x.dtype), 6, axis=-1)
    h = rmsnorm(x, lp['g_pre_mix']) * (1 + sc1) + sh1
    a, st = mixer(h, lp, angs, cache)
    x = x + gt1 * rmsnorm(a, lp['g_post_mix'])
    h = rmsnorm(x, lp['g_pre_ffn']) * (1 + sc2) + sh2
    x = x + gt2 * rmsnorm(conv_ffn(h, lp), lp['g_post_ffn'])
    return x, st


def setup_inputs(seed: int = 0) -> dict:
    key = jax.random.key(seed)
    ks = list(jax.random.split(key, 40))
    f32 = jnp.float32

    def nrm(shape, scale):
        return jax.random.normal(ks.pop(), shape, f32) * scale

    def gain(shape):
        return 1.0 + nrm(shape, 0.01)

    u = jax.random.uniform(ks.pop(), (DEPTH, 2, D_RNN), f32, minval=0.9, maxval=0.999)
    s = u ** (1.0 / RG_C)
    lam = jnp.log(s) - jnp.log1p(-s)
    return {
        'x_prompt': nrm((BATCH, SEQ, D_MODEL), 1.0),
        'x_sample': nrm((DEC_BATCH, DEC_SEQ, D_MODEL), 1.0),
        'c': nrm((DEC_BATCH, D_MODEL), 1.0),
        'cache_gqa_k': nrm((DEC_BATCH, DEPTH, PAST_LEN, KV_A, HD_A), 1.0),
        'cache_gqa_v': nrm((DEC_BATCH, DEPTH, PAST_LEN, KV_A, HD_A), 1.0),
        'cache_mla_ckv': nrm((DEC_BATCH, DEPTH, PAST_LEN, KV_RANK), 1.0),
        'cache_mla_krope': nrm((DEC_BATCH, DEPTH, PAST_LEN, ROPE_B), 1.0),
        'state_rglru_fwd': nrm((DEC_BATCH, DEPTH, D_RNN), 0.5),
        'state_rglru_bwd': nrm((DEC_BATCH, DEPTH, D_RNN), 0.5),
        'c_ctx': nrm((D_MODEL,), 1.0),
        'w_ada': nrm((DEPTH, D_MODEL, 6 * D_MODEL), 0.5 * D_MODEL ** -0.5),
        'b_ada': nrm((DEPTH, 6 * D_MODEL), 0.02),
        'g_pre_mix': gain((DEPTH, D_MODEL)),
        'g_post_mix': gain((DEPTH, D_MODEL)),
        'g_pre_ffn': gain((DEPTH, D_MODEL)),
        'g_post_ffn': gain((DEPTH, D_MODEL)),
        'w_in': nrm((DEPTH, D_MODEL, IN_WIDTH), D_MODEL ** -0.5),
        'g_qa': gain((DEPTH, HD_A)),
        'g_ka': gain((DEPTH, HD_A)),
        'g_ckv': gain((DEPTH, KV_RANK)),
        'w_uk': nrm((DEPTH, KV_RANK, H_B * NOPE_B), KV_RANK ** -0.5),
        'w_uv': nrm((DEPTH, KV_RANK, H_B * VD_B), KV_RANK ** -0.5),
        'conv_rnn_w': nrm((DEPTH, RNN_CONV, D_RNN), RNN_CONV ** -0.5),
        'conv_rnn_b': nrm((DEPTH, D_RNN), 0.01),
        'w_rg': nrm((DEPTH, 2, RNN_BLOCKS, RNN_BS, RNN_BS), RNN_BS ** -0.5),
        'b_rg': nrm((DEPTH, 2, D_RNN), 0.01),
        'w_ig': nrm((DEPTH, 2, RNN_BLOCKS, RNN_BS, RNN_BS), RNN_BS ** -0.5),
        'b_ig': nrm((DEPTH, 2, D_RNN), 0.01),
        'lam': lam,
        'w_oa': nrm((DEPTH, QA_W, D_MODEL), QA_W ** -0.5),
        'w_ob': nrm((DEPTH, H_B * VD_B, D_MODEL), (H_B * VD_B) ** -0.5),
        'w_oc': nrm((DEPTH, D_RNN, D_MODEL), D_RNN ** -0.5),
        'w_out': nrm((DEPTH, D_MODEL, D_MODEL), D_MODEL ** -0.5),
        'w_up': nrm((DEPTH, D_MODEL, 2 * D_FF), D_MODEL ** -0.5),
        'conv_ffn_w': nrm((DEPTH, FFN_CONV, 2 * D_FF), FFN_CONV ** -0.5),
        'conv_ffn_b': nrm((DEPTH, 2 * D_FF), 0.01),
        'w_down': nrm((DEPTH, D_FF, D_MODEL), D_FF ** -0.5),
    }


def reference(x_prompt, x_sample, c, cache_gqa_k, cache_gqa_v, cache_mla_ckv, cache_mla_krope,
              state_rglru_fwd, state_rglru_bwd, c_ctx, w_ada, b_ada, g_pre_mix, g_post_mix,
              g_pre_ffn, g_post_ffn, w_in, g_qa, g_ka, g_ckv, w_uk, w_uv, conv_rnn_w, conv_rnn_b,
              w_rg, b_rg, w_ig, b_ig, lam, w_oa, w_ob, w_oc, w_out, w_up, conv_ffn_w, conv_ffn_b, w_down):
    def params(l):
        return {'g_pre_mix': g_pre_mix[l], 'g_post_mix': g_post_mix[l], 'g_pre_ffn': g_pre_ffn[l],
                'g_post_ffn': g_post_ffn[l], 'w_in': w_in[l], 'g_qa': g_qa[l], 'g_ka': g_ka[l],
                'g_ckv': g_ckv[l], 'w_uk': w_uk[l], 'w_uv': w_uv[l], 'conv_rnn_w': conv_rnn_w[l],
                'conv_rnn_b': conv_rnn_b[l], 'w_rg': w_rg[l], 'b_rg': b_rg[l], 'w_ig': w_ig[l],
                'b_ig': b_ig[l], 'lam': lam[l], 'w_oa': w_oa[l], 'w_ob': w_ob[l], 'w_oc': w_oc[l],
                'w_out': w_out[l], 'w_up': w_up[l], 'conv_ffn_w': conv_ffn_w[l],
                'conv_ffn_b': conv_ffn_b[l], 'w_down': w_down[l]}

    y = x_prompt
    states = []
    for l in range(DEPTH):
        mod = (jax.nn.silu(c_ctx) @ w_ada[l] + b_ada[l])[None, None, :]
        y, st = trunk_layer(y, mod, params(l), None, None)
        states.append(st)
    new_gqa_k = jnp.stack([s[0] for s in states], axis=1)
    new_gqa_v = jnp.stack([s[1] for s in states], axis=1)
    new_mla_ckv = jnp.stack([s[2] for s in states], axis=1)
    new_mla_krope = jnp.stack([s[3] for s in states], axis=1)
    new_rnn_fwd = jnp.stack([s[4] for s in states], axis=1)
    new_rnn_bwd = jnp.stack([s[5] for s in states], axis=1)

    rows = x_sample.shape[1] // GRID_W
    angs = (axial_angles(rows, HD_A), axial_angles(rows, ROPE_B))
    z = x_sample
    for l in range(DEPTH):
        mod = (jax.nn.silu(c) @ w_ada[l] + b_ada[l])[:, None, :]
        cache = (cache_gqa_k[:, l], cache_gqa_v[:, l], cache_mla_ckv[:, l], cache_mla_krope[:, l],
                 state_rglru_fwd[:, l], state_rglru_bwd[:, l])
        z, _ = trunk_layer(z, mod, params(l), angs, cache)

    return (y, z, new_gqa_k, new_gqa_v, new_mla_ckv, new_mla_krope, new_rnn_fwd, new_rnn_bwd)
```

```python
import contextlib
import numpy as np
import concourse.bass as bass
import concourse.mybir as mybir
from concourse.bass_utils import run_bass_kernel_spmd

F32 = mybir.dt.float32
BF16 = mybir.dt.bfloat16
AF = mybir.ActivationFunctionType
ALU = mybir.AluOpType

ENGS = ['pe', 'act', 'dve', 'pool', 'sp']
NDSEM = 12


class _Op:
    __slots__ = ('eng', 'fn', 'deps', 'is_dma', 'sig', 'dsem', 'dprev')

    def __init__(self, eng, fn, deps, is_dma):
        self.eng = eng
        self.fn = fn
        self.deps = deps
        self.is_dma = is_dma
        self.sig = None
        self.dsem = None
        self.dprev = None


class Prog:
    def __init__(self, nc):
        self.nc = nc
        self.ops = []
        self.lastw = {}
        self.rds = {}
        self.pending = {e: set() for e in ENGS}
        self.last_on = {}
        self.dmas_since_bar = []
        self.n_dma = {e: 0 for e in ENGS}
        self.marks = []

    def sb(self, name, shape, dtype):
        return self.nc.alloc_sbuf_tensor('sb_' + name, list(shape), dtype)

    def ps(self, name, shape, dtype=F32):
        return self.nc.alloc_psum_tensor('ps_' + name, list(shape), dtype)

    def _deps(self, eng, reads, writes, is_dma):
        deps = {}
        for k in reads:
            w = self.lastw.get(k)
            if w is not None:
                deps[w] = 'raw'
        for k in writes:
            w = self.lastw.get(k)
            if w is not None and w not in deps:
                deps[w] = 'waw'
            for r in self.rds.get(k, ()):
                if r not in deps:
                    deps[r] = 'war'
        out = []
        for d, kind in deps.items():
            o = self.ops[d]
            if (not o.is_dma) and o.eng == eng and kind != 'raw' and not is_dma:
                continue
            out.append(d)
        for d in self.pending[eng]:
            if d not in deps:
                out.append(d)
        self.pending[eng] = set()
        return out

    def _record(self, idx, reads, writes):
        o = self.ops[idx]
        for k in reads:
            lst = self.rds.setdefault(k, [])
            if not o.is_dma:
                for i, r in enumerate(lst):
                    if (not self.ops[r].is_dma) and self.ops[r].eng == o.eng:
                        lst[i] = idx
                        break
                else:
                    lst.append(idx)
            else:
                lst.append(idx)
        for k in writes:
            self.lastw[k] = idx
            self.rds[k] = []

    def op(self, eng, fn, reads=(), writes=()):
        ex = [k for k in reads if isinstance(k, str) and k.startswith('bank')]
        if ex:
            reads = [k for k in reads if k not in ex]
            writes = list(writes) + ex
        deps = self._deps(eng, reads, writes, False)
        idx = len(self.ops)
        self.ops.append(_Op(eng, fn, deps, False))
        self._record(idx, reads, writes)
        self.last_on[eng] = idx
        return idx

    def dma(self, q, out, in_, reads=(), writes=(), **kw):
        deps = self._deps(q, reads, writes, True)
        idx = len(self.ops)
        o = _Op(q, (out, in_, kw), deps, True)
        j = self.n_dma[q]
        self.n_dma[q] += 1
        o.dsem = (q, j % NDSEM, 16 * (j // NDSEM + 1))
        if j >= NDSEM:
            o.dprev = (q, j % NDSEM, 16 * (j // NDSEM))
        self.ops.append(o)
        self._record(idx, reads, writes)
        self.last_on[q] = idx
        self.dmas_since_bar.append(idx)
        return idx

    def barrier(self):
        s = set(self.last_on.values()) | set(self.dmas_since_bar)
        for e in ENGS:
            self.pending[e] |= s
        self.dmas_since_bar = []

    def emit(self):
        nc = self.nc
        self.barrier()
        self.op('sp', None)
        needed = set()
        for o in self.ops:
            for d in o.deps:
                if not self.ops[d].is_dma:
                    needed.add(d)
        cnt = {e: 0 for e in ENGS}
        for i, o in enumerate(self.ops):
            if i in needed:
                cnt[o.eng] += 1
                o.sig = cnt[o.eng]
        per_eng = {e: [o for o in self.ops if o.eng == e] for e in ENGS}
        with contextlib.ExitStack() as st:
            esem = {e: st.enter_context(nc.semaphore("sg_" + e)) for e in ENGS}
            dsem = {}
            for q in ENGS:
                if self.n_dma[q]:
                    for j in range(min(NDSEM, self.n_dma[q])):
                        dsem[(q, j)] = st.enter_context(nc.semaphore("sd_%s_%d" % (q, j)))
            block = st.enter_context(nc.Block())
            ops = self.ops

            def run(engname):
                def body(eng):
                    waited = {}
                    for o in per_eng[engname]:
                        want = {}
                        for d in o.deps:
                            p = ops[d]
                            if p.is_dma:
                                key = ('d', p.dsem[0], p.dsem[1])
                                val = p.dsem[2]
                            else:
                                key = ('e', p.eng)
                                val = p.sig
                            if val > want.get(key, 0):
                                want[key] = val
                        if o.dprev is not None:
                            key = ('d', o.dprev[0], o.dprev[1])
                            if o.dprev[2] > want.get(key, 0):
                                want[key] = o.dprev[2]
                        for key, val in want.items():
                            if val > waited.get(key, 0):
                                waited[key] = val
                                sem = esem[key[1]] if key[0] == 'e' else dsem[(key[1], key[2])]
                                eng.wait_ge(sem, val)
                        if o.fn is None:
                            continue
                        if o.is_dma:
                            out, in_, kw = o.fn
                            ins = eng.dma_start(out=out, in_=in_, **kw)
                            ins.then_inc(dsem[(o.dsem[0], o.dsem[1])], 16)
                        else:
                            ins = o.fn(eng)
                            if o.sig is not None:
                                ins.then_inc(esem[engname], 1)
                return body

            if per_eng['pe']:
                block.tensor(run('pe'))
            if per_eng['act']:
                block.scalar(run('act'))
            if per_eng['dve']:
                block.vector(run('dve'))
            if per_eng['pool']:
                block.gpsimd(run('pool'))
            if per_eng['sp']:
                block.sync(run('sp'))


D = 1024
DEPTH = 2
T_P, L_P = 512, 256
T_S = 2048
PAST = 512
D_FF = 2816
EPS = 1e-6
O_QA, O_KA, O_VA, O_QB, O_CKV, O_KR, O_XR, O_YR, O_GL = 0, 512, 640, 768, 1536, 1792, 1824, 2848, 3872
IN_W = 6944


def vec_layout():
    off = {}
    n = 0
    for l in range(DEPTH):
        for nm, k in [('gpm', 8), ('gqm', 8), ('gpf', 8), ('gqf', 8), ('bada', 48), ('gqa', 1), ('gka', 1),
                      ('gckv', 2), ('crw', 32), ('crb', 8), ('brg', 16), ('big', 16), ('lam', 16),
                      ('cfw', 132), ('cfb', 44)]:
            off['%s%d' % (nm, l)] = (n, k)
            n += k
    off['cond'] = (n, 16)
    n += 16
    off['st'] = (n, 32)
    n += 32
    return off, n


VOFF, NV = vec_layout()


class Arena:
    def __init__(self, P, nbytes):
        self.t = P.sb('arena', [128, nbytes // 4], F32)
        self.cap = nbytes
        self.off = 0
        self.gen = 0

    def reset(self, to=0):
        self.off = to
        self.gen += 1

    def alloc(self, shape, dt):
        n = 1
        for s in shape:
            n *= s
        b = n * (2 if dt == BF16 else 4)
        b32 = (b + 63) // 64 * 64
        a = self.off
        self.off += b32
        assert self.off <= self.cap, ("arena overflow", self.off, self.cap)
        v = self.t[:, a // 4:(a + b32) // 4]
        if dt != F32:
            v = v.bitcast(dt)
        v = v[:, 0:n]
        if len(shape) > 1:
            names = ['d%d' % i for i in range(len(shape))]
            kw = {names[i]: shape[i] for i in range(1, len(shape))}
            v = v.rearrange('p (%s) -> p %s' % (' '.join(names), ' '.join(names)), **kw)
        return v


def build_program():
    import os
    STOP = os.environ.get('MK_STOP', '')
    DBG = bool(os.environ.get('MK_DBG', ''))
    nc = bass.Bass("TRN2", target_bir_lowering=False)
    P = Prog(nc)

    def din(name, shape):
        return nc.dram_tensor(name, list(shape), F32, kind="ExternalInput").ap()

    def dout(name, shape):
        return nc.dram_tensor(name, list(shape), F32, kind="ExternalOutput").ap()

    def dscr(name, shape, dt):
        return nc.dram_tensor(name, list(shape), dt, kind=("ExternalOutput" if DBG else "Internal")).ap()

    class _Stop(Exception):
        pass

    def stop_if(tag):
        P.marks.append((tag, len(P.ops)))
        print('arena', tag, AR.off)
        if STOP == tag:
            raise _Stop()

    d_x = [din('xp', [T_P, D]), din('xs', [T_S, D])]
    d_vecs = din('vecs', [128, NV])
    d_consts = din('consts', [128, 5, 128])
    d_cosA = din('cosA', [128, T_S])
    d_sinA = din('sinA', [128, T_S])
    d_cosB = din('cosB', [96, T_S])
    d_sinB = din('sinB', [96, T_S])
    d_ck = din('ck', [DEPTH, PAST, 128])
    d_cv = din('cv', [DEPTH, PAST, 128])
    d_cckv = din('cckv', [DEPTH, PAST, 256])
    d_ckr = din('ckr', [DEPTH, PAST, 32])
    d_wada = din('wada', [DEPTH, 12, 128, 8, 512])
    d_win = din('win', [DEPTH, 128, 8, IN_W])
    d_wqa = din('wqa', [DEPTH, 128, 8, 512])
    d_wkr = din('wkr', [DEPTH, 128, 8, 96])
    d_wmg = din('wmg', [DEPTH, 8, 128, 40, 128])
    d_wout = din('wout', [DEPTH, 2, 128, 8, 512])
    d_wup = din('wup', [DEPTH, 22, 128, 8, 256])
    d_wdn = din('wdn', [DEPTH, 4, 128, 22, 256])
    d_wuk = din('wuk', [DEPTH, 128, 2, 512])
    d_wuv = din('wuv', [DEPTH, 128, 2, 512])
    d_wrg = din('wrg', [DEPTH, 128, 2, 8, 128])
    d_wig = din('wig', [DEPTH, 128, 2, 8, 128])

    d_y = [dout('y_p', [T_P, D]), dout('y_s', [T_S, D])]
    d_nk = dout('nk', [2, DEPTH, L_P, 128])
    d_nv = dout('nv', [2, DEPTH, L_P, 128])
    d_nckv = dout('nckv', [2, DEPTH, L_P, 256])
    d_nkr = dout('nkr', [2, DEPTH, L_P, 32])
    d_nf = dout('nf', [2, DEPTH, D])
    d_nb = dout('nb', [2, DEPTH, D])

    GR = [dict(gi=0, T=T_P, nseq=2, L=L_P, cache=0, rope=False, outs=True),
          dict(gi=1, T=T_S, nseq=1, L=T_S, cache=PAST, rope=True, outs=False)]
    xA = [dscr('xA%d' % g['gi'], [128, 8, g['T']], F32) for g in GR]
    xB = [dscr('xB%d' % g['gi'], [128, 8, g['T']], F32) for g in GR]
    s_oc = [dscr('soc%d' % g['gi'], [128, 8, g['T']], BF16) for g in GR]
    s_oa = [dscr('soa%d' % g['gi'], [128, 4, g['T']], BF16) for g in GR]
    s_ob = [dscr('sob%d' % g['gi'], [128, 4, g['T']], BF16) for g in GR]

    cst_f = P.sb('cst_f', [128, 5, 128], F32)
    cst_b = P.sb('cst_b', [128, 5, 128], BF16)
    vecs = P.sb('vecs', [128, NV], F32)
    modraw = P.sb('modraw', [128, DEPTH, 48, 2], F32)
    modv = P.sb('modv', [128, DEPTH, 2, 4, 8], F32)
    clv = P.sb('clv', [128, DEPTH, 16], F32)
    hT = P.sb('hT', [128, 8, T_S], BF16)
    WB = 12288
    wbufs = [P.sb('wbuf%d' % i, [128, WB // 2], BF16) for i in range(3)]
    PS = P.ps('psall', [128, 8 * 512], F32)
    PS3 = PS[:, :].rearrange('p (b n) -> p b n', b=8)
    banks = [PS3[:, i, :] for i in range(8)]
    AR = Arena(P, 130 * 1024)
    print("sbuf remaining after static", nc.sbuf_bytes_remaining)

    ones_b = cst_b[:, 0, :]
    bones_b = cst_b[:, 1, :]
    ident_f = cst_f[:, 2, :]
    RA_b = cst_b[:, 3, :]
    RB_b = cst_b[0:96, 4, 0:96]

    st = dict(bank=0, wb=0, uid=0)

    def nbank():
        if st.get('lo'):
            return nbank_lo()
        i = st['bank']
        st['bank'] = (i + 1) % 8
        return banks[i], 'bank%d' % i

    def nwb():
        i = st['wb']
        st['wb'] = (i + 1) % 3
        return wbufs[i], 'wbuf%d' % i

    def uid(s):
        st['uid'] += 1
        return '%s_%d' % (s, st['uid'])

    def V(name, l=None):
        a, k = VOFF[name if l is None else '%s%d' % (name, l)]
        return vecs[:, a:a + k]

    wcache = {}

    def wload(src, kc, m, ck=None):
        wb, wk = nwb()
        assert kc * m * 2 <= WB
        v = wb[:, 0:kc * m].rearrange('p (k m) -> p k m', k=kc)
        if ck is not None and ck in wcache:
            P.dma('pool', v, wcache[ck], reads=[], writes=[wk])
            return v, wk
        P.dma('pool', v, src, reads=[], writes=[wk])
        if ck is not None:
            wcache[ck] = dscr('wc_' + ck, [128, kc, m], BF16)
            P.dma('sp', wcache[ck], v, reads=[wk], writes=[])
        return v, wk

    try:
        P.dma('sp', cst_f[:], d_consts, writes=['cst_f'])
        P.dma('pool', cst_b[:], d_consts, writes=['cst_b'])
        P.dma('sp', vecs[:], d_vecs, writes=['vecs'])
        scb = P.sb('scb', [128, 16], BF16)
        P.op('act', lambda e: e.activation(out=scb[:], in_=V('cond'), func=AF.Silu), reads=['vecs'], writes=['scb'])
        def cl_ops(l):
            tz = [P.sb(uid('tz'), [128, 16], F32) for _ in range(6)]
            lam = V('lam', l)
            kz = uid('kz')
            seq = [
                ('dve', lambda e: e.tensor_scalar(out=tz[0][:], in0=lam, scalar1=-1.0, scalar2=None, op0=ALU.mult)),
                ('dve', lambda e: e.tensor_tensor(out=tz[0][:], in0=tz[0][:], in1=lam, op=ALU.max)),
                ('act', lambda e: e.activation(out=tz[1][:], in_=tz[0][:], func=AF.Exp, scale=-1.0)),
                ('dve', lambda e: e.tensor_scalar(out=tz[2][:], in0=tz[1][:], scalar1=2.0, scalar2=None, op0=ALU.add)),
                ('dve', lambda e: e.reciprocal(out=tz[3][:], in_=tz[2][:])),
                ('dve', lambda e: e.tensor_tensor(out=tz[2][:], in0=tz[1][:], in1=tz[3][:], op=ALU.mult)),
                ('dve', lambda e: e.tensor_tensor(out=tz[3][:], in0=tz[2][:], in1=tz[2][:], op=ALU.mult)),
                ('dve', lambda e: e.tensor_scalar(out=tz[4][:], in0=tz[3][:], scalar1=1.0 / 9, scalar2=1.0 / 7, op0=ALU.mult, op1=ALU.add)),
                ('dve', lambda e: e.tensor_tensor(out=tz[5][:], in0=tz[4][:], in1=tz[3][:], op=ALU.mult)),
                ('dve', lambda e: e.tensor_scalar(out=tz[4][:], in0=tz[5][:], scalar1=1.0 / 5, scalar2=None, op0=ALU.add)),
                ('dve', lambda e: e.tensor_tensor(out=tz[5][:], in0=tz[4][:], in1=tz[3][:], op=ALU.mult)),
                ('dve', lambda e: e.tensor_scalar(out=tz[4][:], in0=tz[5][:], scalar1=1.0 / 3, scalar2=None, op0=ALU.add)),
                ('dve', lambda e: e.tensor_tensor(out=tz[5][:], in0=tz[4][:], in1=tz[3][:], op=ALU.mult)),
                ('dve', lambda e: e.tensor_scalar(out=tz[4][:], in0=tz[5][:], scalar1=1.0, scalar2=None, op0=ALU.add)),
                ('dve', lambda e: e.tensor_tensor(out=tz[5][:], in0=tz[4][:], in1=tz[2][:], op=ALU.mult)),
                ('dve', lambda e: e.tensor_scalar(out=tz[0][:], in0=lam, scalar1=-1.0, scalar2=0.0, op0=ALU.mult, op1=ALU.max)),
                ('dve', lambda e: e.scalar_tensor_tensor(out=tz[1][:], in0=tz[5][:], scalar=2.0, in1=tz[0][:], op0=ALU.mult, op1=ALU.add)),
                ('dve', lambda e: e.tensor_scalar(out=clv[:, l, :], in0=tz[1][:], scalar1=-8.0, scalar2=None, op0=ALU.mult)),
            ]
            for en, fn in seq:
                P.op(en, fn, reads=[kz, 'vecs'], writes=[kz, 'clv%d' % l])

        DEFER = []

        def ada_piece(l, pc):
            w, wk = wload(d_wada[l][pc], 8, 512)
            bk, bkk = nbank()
            for jj in range(4):
                for k in range(8):
                    P.op('pe', lambda e, jj=jj, k=k: e.matmul(
                        bk[:, jj * 2:(jj + 1) * 2], w[:, k, jj * 128:(jj + 1) * 128], scb[:, k * 2:(k + 1) * 2],
                        start=(k == 0), stop=(k == 7)), reads=[wk, 'scb'], writes=[bkk])
            P.op('dve', lambda e: e.tensor_tensor(
                out=modraw[:, l, pc * 4:(pc + 1) * 4, :], in0=bk[:, 0:8].rearrange('p (j g) -> p j g', g=2),
                in1=V('bada', l)[:, pc * 4:(pc + 1) * 4].unsqueeze(2).broadcast_to([128, 4, 2]), op=ALU.add),
                reads=[bkk, 'vecs'], writes=['modraw%d' % l])

        def ada_finish(l):
            for g in range(2):
                P.op('dve', lambda e, g=g: e.scalar_tensor_tensor(
                    out=modv[:, l, g, 0, :], in0=modraw[:, l, 8:16, g], scalar=1.0, in1=V('gpm', l),
                    op0=ALU.add, op1=ALU.mult), reads=['modraw%d' % l, 'vecs'], writes=['modv%d' % l])
                P.op('dve', lambda e, g=g: e.tensor_tensor(
                    out=modv[:, l, g, 1, :], in0=modraw[:, l, 16:24, g], in1=V('gqm', l), op=ALU.mult),
                    reads=['modraw%d' % l, 'vecs'], writes=['modv%d' % l])
                P.op('dve', lambda e, g=g: e.scalar_tensor_tensor(
                    out=modv[:, l, g, 2, :], in0=modraw[:, l, 32:40, g], scalar=1.0, in1=V('gpf', l),
                    op0=ALU.add, op1=ALU.mult), reads=['modraw%d' % l, 'vecs'], writes=['modv%d' % l])
                P.op('dve', lambda e, g=g: e.tensor_tensor(
                    out=modv[:, l, g, 3, :], in0=modraw[:, l, 40:48, g], in1=V('gqf', l), op=ALU.mult),
                    reads=['modraw%d' % l, 'vecs'], writes=['modv%d' % l])
            cl_ops(l)

        for pc in range(12):
            ada_piece(0, pc)
        ada_finish(0)
        for pc in range(12):
            DEFER.append(lambda pc=pc: ada_piece(1, pc))
        DEFER.append(lambda: ada_finish(1))

        def run_deferred(k_):
            for _ in range(k_):
                if DEFER:
                    DEFER.pop(0)()
        stop_if('p0')

        AR.reset()
        xin = [AR.alloc([D], F32) for _ in range(4)]
        xfm = [AR.alloc([8, 128], F32) for _ in range(4)]
        pools = {}

        def sc(name, shape, dt, n=2):
            key = (AR.gen, name)
            if key not in pools:
                pools[key] = [[AR.alloc(shape, dt) for _ in range(n)], 0]
            pl = pools[key]
            i = pl[1] % len(pl[0])
            pl[1] += 1
            return pl[0][i], '%s_g%d_%d' % (name, AR.gen, i)
        nblk = 0
        for g in GR:
            gi = g['gi']
            for tb in range(g['T'] // 128):
                b = nblk % 4
                nblk += 1
                P.dma('sp', xin[b], d_x[gi][tb * 128:(tb + 1) * 128, :], writes=['xin%d' % b])
                for half in range(2):
                    bk, bkk = nbank()
                    for cc in range(4):
                        c = half * 4 + cc
                        P.op('pe', lambda e, bk=bk, cc=cc, c=c, b=b: e.transpose(
                            bk[:, cc * 128:(cc + 1) * 128], xin[b][:, c * 128:(c + 1) * 128], ident_f),
                            reads=['xin%d' % b, 'cst_f'], writes=[bkk])
                    eng = 'act' if half == 0 else 'dve'
                    if eng == 'act':
                        P.op('act', lambda e, bk=bk, b=b, half=half: e.activation(
                            out=xfm[b][:, half * 4:(half + 1) * 4, :], in_=bk[:, :].rearrange('p (c t) -> p c t', c=4),
                            func=AF.Copy), reads=[bkk], writes=['xfm%d_%d' % (b, half)])
                    else:
                        P.op('dve', lambda e, bk=bk, b=b, half=half: e.tensor_copy(
                            out=xfm[b][:, half * 4:(half + 1) * 4, :], in_=bk[:, :].rearrange('p (c t) -> p c t', c=4)),
                            reads=[bkk], writes=['xfm%d_%d' % (b, half)])
                P.dma('pool', xA[gi][:, :, tb * 128:(tb + 1) * 128], xfm[b],
                      reads=['xfm%d_0' % b, 'xfm%d_1' % b], writes=[])
        P.barrier()

        def rstd_from_sq(sq, C, N, dtot, sqkeys, nb=1):
            bk, bkk = nbank()
            for c in range(C):
                P.op('pe', lambda e, bk=bk, c=c: e.matmul(bk[:, 0:N], ones_b, sq[:, c, 0:N], start=(c == 0), stop=(c == C - 1)),
                     reads=[sqkeys[c] if isinstance(sqkeys, list) else sqkeys, 'cst_b'], writes=[bkk])
            std, k1 = sc('std%d' % nb, [512], F32, nb)
            P.op('act', lambda e: e.activation(out=std[:, 0:N], in_=bk[:, 0:N], func=AF.Sqrt, bias=EPS, scale=1.0 / dtot),
                 reads=[bkk], writes=[k1])
            rs, k2 = sc('rstd%d' % nb, [512], F32, nb)
            P.op('dve', lambda e: e.reciprocal(out=rs[:, 0:N], in_=std[:, 0:N]), reads=[k1], writes=[k2])
            return rs, k2

        def norm_mod(xt, xk, N, l, gi, which, out_fn, out_keys, nb=1):
            sq, ks = sc('sq8_%d' % nb, [8, 512], BF16, nb)
            P.op('act', lambda e: e.activation(out=sq[:, :, 0:N], in_=xt[:, :, 0:N], func=AF.Square), reads=[xk], writes=[ks])
            rs, rk = rstd_from_sq(sq, 8, N, D, ks, nb)
            P.op('dve', lambda e: e.tensor_tensor(out=xt[:, :, 0:N], in0=xt[:, :, 0:N],
                                                  in1=rs[:, 0:N].unsqueeze(1).broadcast_to([128, 8, N]), op=ALU.mult),
                 reads=[xk, rk], writes=[xk])
            ai = 0 if which == 1 else 2
            sec = 0 if which == 1 else 24
            for c in range(8):
                P.op('act', lambda e, c=c: e.activation(out=out_fn(c), in_=xt[:, c, 0:N], func=AF.Identity,
                                                         scale=modv[:, l, gi, ai, c:c + 1],
                                                         bias=modraw[:, l, sec + c, gi:gi + 1]),
                     reads=[xk, 'modv%d' % l, 'modraw%d' % l], writes=[out_keys[c]])

        def epilogue(outf, ok, sq, sqk, xt, xk, N, xoff, l, gi, which):
            rs, rk = rstd_from_sq(sq, 8, N, D, sqk)
            P.op('dve', lambda e: e.tensor_tensor(out=outf[:, :, 0:N], in0=outf[:, :, 0:N],
                                                  in1=rs[:, 0:N].unsqueeze(1).broadcast_to([128, 8, N]), op=ALU.mult),
                 reads=[ok, rk], writes=[ok])
            gidx = 1 if which == 1 else 3
            for c in range(8):
                P.op('dve', lambda e, c=c: e.scalar_tensor_tensor(
                    out=xt[:, c, xoff:xoff + N], in0=outf[:, c, 0:N], scalar=modv[:, l, gi, gidx, c:c + 1],
                    in1=xt[:, c, xoff:xoff + N], op0=ALU.mult, op1=ALU.add), reads=[ok, xk, 'modv%d' % l], writes=[xk])

        def rms_heads(bk, bkk, N, rows, gvec, hd, lhs_ones):
            sq, ks = sc('sqh', [512], BF16, 1)
            P.op('act', lambda e: e.activation(out=sq[0:rows, 0:N], in_=bk[0:rows, 0:N], func=AF.Square), reads=[bkk], writes=[ks])
            b2, b2k = nbank()
            P.op('pe', lambda e: e.matmul(b2[0:rows, 0:N], lhs_ones[0:rows, 0:rows], sq[0:rows, 0:N], start=True, stop=True),
                 reads=[ks, 'cst_b'], writes=[b2k])
            std, k1 = sc('stdh', [512], F32, 1)
            P.op('act', lambda e: e.activation(out=std[0:rows, 0:N], in_=b2[0:rows, 0:N], func=AF.Sqrt, bias=EPS, scale=1.0 / hd),
                 reads=[b2k], writes=[k1])
            rs, k2 = sc('rstdh', [512], F32, 1)
            P.op('dve', lambda e: e.reciprocal(out=rs[0:rows, 0:N], in_=std[0:rows, 0:N]), reads=[k1], writes=[k2])
            kn, kk = sc('kn', [512], F32, 1)
            P.op('dve', lambda e: e.scalar_tensor_tensor(out=kn[0:rows, 0:N], in0=bk[0:rows, 0:N], scalar=gvec,
                                                         in1=rs[0:rows, 0:N], op0=ALU.mult, op1=ALU.mult),
                 reads=[bkk, k2, 'vecs'], writes=[kk])
            return kn, kk

        def rope_apply(src, sk, N, r0, r1, R_lhsT, cos_t, sin_t, tk, out_ap, out_key, outs=None):
            sb16, k16 = sc('s16', [512], BF16, 1)
            P.op('act', lambda e: e.activation(out=sb16[0:r1, 0:N], in_=src[0:r1, 0:N], func=AF.Copy), reads=[sk], writes=[k16])
            bk, bkk = nbank()
            P.op('pe', lambda e: e.matmul(bk[0:r1, 0:N], R_lhsT, sb16[0:r1, 0:N], start=True, stop=True),
                 reads=[k16, 'cst_b'], writes=[bkk])
            t1, k1 = sc('t1', [512], F32, 1)
            t2, k2 = sc('t2', [512], F32, 1)
            P.op('dve', lambda e: e.tensor_tensor(out=t1[r0:r1, 0:N], in0=src[r0:r1, 0:N], in1=cos_t[r0:r1, 0:N], op=ALU.mult),
                 reads=[sk, tk], writes=[k1])
            P.op('dve', lambda e: e.tensor_tensor(out=t2[r0:r1, 0:N], in0=bk[r0:r1, 0:N], in1=sin_t[r0:r1, 0:N], op=ALU.mult),
                 reads=[bkk, tk], writes=[k2])
            for (ra_, rb_, oap) in (outs or [(r0, r1, out_ap)]):
                P.op('dve', lambda e, ra_=ra_, rb_=rb_, oap=oap: e.tensor_tensor(out=oap, in0=t1[ra_:rb_, 0:N], in1=t2[ra_:rb_, 0:N], op=ALU.add),
                     reads=[k1, k2], writes=[out_key])

        def hkeys(tt):
            return ['hT%d_%d' % (tt, c) for c in range(8)]

        def phase1(g, l, xsrc):
            gi, T = g['gi'], g['T']
            AR.reset()
            for tt in range(T // 512):
                xt, xk = sc('xt', [8, 512], F32)
                P.dma('sp', xt, xsrc[gi][:, :, tt * 512:(tt + 1) * 512], writes=[xk])
                norm_mod(xt, xk, 512, l, gi, 1, lambda c, tt=tt: hT[:, c, tt * 512:(tt + 1) * 512], hkeys(tt), nb=2)

        def phase2(g, l):
            gi, T, nseq, L, outs = g['gi'], g['T'], g['nseq'], g['L'], g['outs']
            AR.reset()
            ntile = T // 512
            wrg = AR.alloc([2, 8, 128], BF16)
            wig = AR.alloc([2, 8, 128], BF16)
            P.dma('pool', wrg, d_wrg[l], writes=['wrg'])
            P.dma('pool', wig, d_wig[l], writes=['wig'])
            xpad2 = [AR.alloc([nseq, L + 3], F32) for _ in range(2)]
            uf2 = [AR.alloc([T], F32) for _ in range(2)]
            ubf2 = [AR.alloc([T], BF16) for _ in range(2)]
            ra2 = [AR.alloc([T], F32) for _ in range(2)]
            ib2 = [AR.alloc([T], F32) for _ in range(2)]
            tm = AR.alloc([T], F32)
            hh = [AR.alloc([T], F32) for _ in range(2)]
            gy = AR.alloc([T], F32)
            stout = AR.alloc([32], F32)
            stT = AR.alloc([128], F32)
            for pb in range(2):
                P.op('dve', lambda e, pb=pb: e.memset(xpad2[pb][:, :, 0:2], 0.0), writes=['xpad%d' % pb])
                P.op('dve', lambda e, pb=pb: e.memset(xpad2[pb][:, :, L + 2:L + 3], 0.0), writes=['xpad%d' % pb])
            crw = V('crw', l)
            ncol = slice(0, 128)

            def stageA(n):
                pb = n % 2
                xpad, uf, ubf = xpad2[pb], uf2[pb], ubf2[pb]
                xpk, ufk, ubk = 'xpad%d' % pb, 'u%d' % pb, 'ubf%d' % pb
                wxr, wxrk = sc('wxr', [8, 128], BF16, 2)
                P.dma('pool', wxr, d_win[l][:, :, O_XR + n * 128:O_XR + (n + 1) * 128], writes=[wxrk])
                for tt in range(ntile):
                    bk, bkk = nbank()
                    for k in range(8):
                        P.op('pe', lambda e, bk=bk, k=k, tt=tt: e.matmul(
                            bk[:, 0:512], wxr[:, k, ncol], hT[:, k, tt * 512:(tt + 1) * 512], start=(k == 0), stop=(k == 7)),
                            reads=[wxrk] + hkeys(tt), writes=[bkk])
                    if nseq == 1:
                        P.op('act', lambda e, bk=bk, tt=tt: e.activation(out=xpad[:, 0, 2 + tt * 512:2 + (tt + 1) * 512],
                                                                         in_=bk[:, 0:512], func=AF.Copy),
                             reads=[bkk], writes=[xpk])
                    else:
                        P.op('act', lambda e, bk=bk: e.activation(out=xpad[:, :, 2:2 + L],
                                                                  in_=bk[:, 0:512].rearrange('p (s t) -> p s t', s=nseq),
                                                                  func=AF.Copy), reads=[bkk], writes=[xpk])
                u3 = uf.rearrange('p (s t) -> p s t', s=nseq)
                P.op('dve', lambda e: e.tensor_scalar(out=u3, in0=xpad[:, :, 0:L], scalar1=crw[:, n:n + 1],
                                                      scalar2=V('crb', l)[:, n:n + 1], op0=ALU.mult, op1=ALU.add),
                     reads=[xpk, 'vecs'], writes=[ufk])
                for k in range(1, 4):
                    P.op('dve', lambda e, k=k: e.scalar_tensor_tensor(
                        out=u3, in0=xpad[:, :, k:k + L], scalar=crw[:, k * 8 + n:k * 8 + n + 1], in1=u3,
                        op0=ALU.mult, op1=ALU.add), reads=[xpk, ufk, 'vecs'], writes=[ufk])
                P.op('act', lambda e: e.activation(out=ubf, in_=uf, func=AF.Copy), reads=[ufk], writes=[ubk])

            def stageY(n):
                wyr, wyrk = sc('wyr', [8, 128], BF16, 2)
                P.dma('pool', wyr, d_win[l][:, :, O_YR + n * 128:O_YR + (n + 1) * 128], writes=[wyrk])
                for tt in range(ntile):
                    bk, bkk = nbank()
                    for k in range(8):
                        P.op('pe', lambda e, bk=bk, k=k, tt=tt: e.matmul(
                            bk[:, 0:512], wyr[:, k, ncol], hT[:, k, tt * 512:(tt + 1) * 512], start=(k == 0), stop=(k == 7)),
                            reads=[wyrk] + hkeys(tt), writes=[bkk])
                    P.op('act', lambda e, bk=bk, tt=tt: e.activation(out=gy[:, tt * 512:(tt + 1) * 512], in_=bk[:, 0:512],
                                                                     func=AF.Gelu_apprx_tanh), reads=[bkk], writes=['gy'])

            def stageB(n, d, part):
                pb = n % 2
                uf, ubf = uf2[pb], ubf2[pb]
                ufk, ubk = 'u%d' % pb, 'ubf%d' % pb
                dn = d * 8 + n
                ra, ib = ra2[d], ib2[d]
                rak, ibk = 'ra%d' % d, 'ib%d' % d
                for tt in (range(ntile) if part == 0 else []):
                    ts_ = slice(tt * 512, (tt + 1) * 512)
                    for (wg, bname, dst, dk) in ((wrg, 'brg', ra, rak), (wig, 'big', ib, ibk)):
                        bk, bkk = nbank()
                        P.op('pe', lambda e, bk=bk, wg=wg, ts_=ts_: e.matmul(
                            bk[:, 0:512], wg[:, d, n, :], ubf[:, ts_], start=True, stop=True),
                            reads=['wrg', 'wig', ubk], writes=[bkk])
                        P.op('act', lambda e, bk=bk, dst=dst, ts_=ts_, bname=bname: e.activation(
                            out=dst[:, ts_], in_=bk[:, 0:512], func=AF.Sigmoid, bias=V(bname, l)[:, dn:dn + 1]),
                            reads=[bkk, 'vecs'], writes=[dk])
                if part == 0:
                    return
                if part == 1:
                    P.op('act', lambda e: e.activation(out=ra, in_=ra, func=AF.Exp, scale=clv[:, l, dn:dn + 1]),
                         reads=[rak, 'clv%d' % l], writes=[rak])
                    return
                P.op('dve', lambda e: e.tensor_tensor(out=tm, in0=ra, in1=ra, op=ALU.mult), reads=[rak], writes=['tm'])
                P.op('act', lambda e: e.activation(out=tm, in_=tm, func=AF.Sqrt, scale=-1.0, bias=1.0), reads=['tm'], writes=['tm'])
                P.op('dve', lambda e: e.tensor_tensor(out=ib, in0=ib, in1=uf, op=ALU.mult), reads=[ibk, ufk], writes=[ibk])
                P.op('dve', lambda e: e.tensor_tensor(out=ib, in0=ib, in1=tm, op=ALU.mult), reads=[ibk, 'tm'], writes=[ibk])
                for s_ in range(nseq):
                    init = V('st')[:, (l * 2 + d) * 8 + n:(l * 2 + d) * 8 + n + 1] if g['cache'] else 0.0
                    sl = slice(s_ * L, (s_ + 1) * L)
                    if d == 0:
                        P.op('dve', lambda e, sl=sl, init=init: e.tensor_tensor_scan(
                            out=hh[0][:, sl], data0=ra[:, sl], data1=ib[:, sl], initial=init, op0=ALU.mult, op1=ALU.add),
                            reads=[rak, ibk, 'vecs'], writes=['hh0'])
                    else:
                        P.op('dve', lambda e, sl=sl, init=init: e.tensor_tensor_scan(
                            out=hh[1][:, sl][:, ::-1], data0=ra[:, sl][:, ::-1], data1=ib[:, sl][:, ::-1], initial=init,
                            op0=ALU.mult, op1=ALU.add), reads=[rak, ibk, 'vecs'], writes=['hh1'])

            def stageC(n):
                if outs:
                    for s_ in range(nseq):
                        P.op('dve', lambda e, s_=s_: e.tensor_copy(out=stout[:, (s_ * 2) * 8 + n:(s_ * 2) * 8 + n + 1],
                                                                   in_=hh[0][:, (s_ + 1) * L - 1:(s_ + 1) * L]),
                             reads=['hh0'], writes=['stout'])
                        P.op('dve', lambda e, s_=s_: e.tensor_copy(out=stout[:, (s_ * 2 + 1) * 8 + n:(s_ * 2 + 1) * 8 + n + 1],
                                                                   in_=hh[1][:, s_ * L:s_ * L + 1]),
                             reads=['hh1'], writes=['stout'])
                P.op('dve', lambda e: e.tensor_tensor(out=hh[0], in0=hh[0], in1=hh[1], op=ALU.add), reads=['hh0', 'hh1'], writes=['hh0'])
                ocb, ock = sc('ocb', [T], BF16, 1)
                P.op('dve', lambda e: e.tensor_tensor(out=ocb, in0=gy, in1=hh[0], op=ALU.mult), reads=['gy', 'hh0'], writes=[ock])
                P.dma('sp', s_oc[gi][:, n, :], ocb, reads=[ock], writes=[])

            stageA(0)
            for n in range(8):
                if n + 1 < 8:
                    stageA(n + 1)
                stageY(n)
                stageB(n, 0, 0)
                stageB(n, 1, 0)
                stageB(n, 0, 1)
                stageB(n, 1, 1)
                stageB(n, 0, 2)
                stageB(n, 1, 2)
                stageC(n)
                run_deferred(2)
            if outs:
                bk, bkk = nbank()
                P.op('pe', lambda e: e.transpose(bk[0:32, 0:128], stout, ident_f), reads=['stout', 'cst_f'], writes=[bkk])
                P.op('act', lambda e: e.activation(out=stT[0:32, :], in_=bk[0:32, 0:128], func=AF.Copy), reads=[bkk], writes=['stT'])
                for s_ in range(nseq):
                    for d in range(2):
                        dst = (d_nf if d == 0 else d_nb)[s_, l, :].rearrange('(n p) -> n p', p=128)
                        r0 = (s_ * 2 + d) * 8
                        P.dma('sp', dst, stT[r0:r0 + 8, :], reads=['stT'], writes=[])

        KS = {}

        def phase3(g, l):
            gi, T, nseq, L, cache, rope, outs = g['gi'], g['T'], g['nseq'], g['L'], g['cache'], g['rope'], g['outs']
            AR.reset()
            KS['wq'] = (wload(d_wqa[l], 8, 512), wload(d_win[l][:, :, O_QB:O_QB + 768], 8, 768))
            S_tot = cache + T
            NKC = S_tot // 128
            kz0 = AR.alloc([S_tot], BF16)
            kz1 = AR.alloc([S_tot], BF16)
            va = AR.alloc([NKC, 132], BF16)
            kbT = AR.alloc([8, S_tot], BF16)
            vb = AR.alloc([NKC, 528], BF16)
            KS.update(kz0=kz0, kz1=kz1, va=va, kbT=kbT, vb=vb, mark=AR.off)
            wkv = AR.alloc([8, 256], BF16)
            wckv = AR.alloc([8, 256], BF16)
            wkr = AR.alloc([8, 96], BF16)
            wuk = AR.alloc([2, 512], BF16)
            wuv = AR.alloc([2, 512], BF16)
            P.dma('pool', wkv, d_win[l][:, :, O_KA:O_KA + 256], writes=['wkv'])
            P.dma('pool', wckv, d_win[l][:, :, O_CKV:O_CKV + 256], writes=['wckv'])
            P.dma('pool', wkr, d_wkr[l], writes=['wkr'])
            P.dma('pool', wuk, d_wuk[l], writes=['wuk'])
            P.dma('pool', wuv, d_wuv[l], writes=['wuv'])
            P.op('dve', lambda e: e.memset(kz0, 0.0), writes=['kaT'])
            P.op('dve', lambda e: e.memset(kz1, 0.0), writes=['kaT'])
            P.op('dve', lambda e: e.memset(va, 1.0), writes=['va'])
            P.op('dve', lambda e: e.memset(vb, 1.0), writes=['vb'])
            KS['mark_w'] = AR.off
            stop_if('p3a')
            va4 = va.rearrange('p j (g e) -> p j g e', g=2)
            vb4 = vb.rearrange('p j (h e) -> p j h e', h=8)

            def kside_from(cnb, cnbk, krr, krrk, col0, N, kc0):
                for h in range(8):
                    bk, bkk = nbank()
                    for j in range(2):
                        P.op('pe', lambda e, bk=bk, j=j, h=h: e.matmul(
                            bk[0:64, 0:N], wuk[:, j, h * 64:(h + 1) * 64], cnb[:, j, 0:N], start=(j == 0), stop=(j == 1)),
                            reads=['wuk', cnbk], writes=[bkk])
                    if h % 2 == 0:
                        P.op('act', lambda e, bk=bk, h=h: e.activation(out=kbT[0:64, h, col0:col0 + N], in_=bk[0:64, 0:N], func=AF.Copy),
                             reads=[bkk], writes=['kbT'])
                    else:
                        P.op('dve', lambda e, bk=bk, h=h: e.tensor_copy(out=kbT[0:64, h, col0:col0 + N], in_=bk[0:64, 0:N]),
                             reads=[bkk], writes=['kbT'])
                P.op('dve', lambda e: e.tensor_copy(out=kbT[64:96, :, col0:col0 + N],
                                                    in_=krr[64:96, 0:N].unsqueeze(1).broadcast_to([32, 8, N])),
                     reads=[krrk], writes=['kbT'])
                for blk in range(N // 128):
                    bk, bkk = nbank()
                    for j in range(2):
                        P.op('pe', lambda e, bk=bk, j=j, blk=blk: e.matmul(
                            bk[:, 0:512], cnb[:, j, blk * 128:(blk + 1) * 128], wuv[:, j, :], start=(j == 0), stop=(j == 1)),
                            reads=['wuv', cnbk], writes=[bkk])
                    P.op('act', lambda e, bk=bk, blk=blk: e.activation(
                        out=vb4[:, kc0 + blk, :, 0:64], in_=bk[:, 0:512].rearrange('p (h d) -> p h d', h=8), func=AF.Copy),
                        reads=[bkk], writes=['vb'])

            if cache:
                stg_k = AR.alloc([4, 128], F32)
                stg_c = AR.alloc([4, 256], F32)
                stg_r = AR.alloc([4, 96], F32)
                cnb_c = AR.alloc([2, 512], BF16)
                krr_c = AR.alloc([512], F32)
                P.dma('sp', stg_k, d_ck[l].rearrange('(j p) f -> p j f', p=128), writes=['stg_k'])
                P.dma('sp', stg_c, d_cckv[l].rearrange('(j p) f -> p j f', p=128), writes=['stg_c'])
                P.op('dve', lambda e: e.memset(stg_r, 0.0), writes=['stg_r'])
                P.dma('sp', stg_r[:, :, 64:96], d_ckr[l].rearrange('(j p) f -> p j f', p=128), writes=['stg_r'])
                for gg_ in range(2):
                    P.dma('pool', va4[:, 0:4, gg_, 0:64], d_cv[l].rearrange('(j p) (g d) -> p j g d', p=128, g=2)[:, :, gg_, :], writes=['va'])
                bk, bkk = nbank()
                for j in range(4):
                    P.op('pe', lambda e, bk=bk, j=j: e.transpose(bk[:, j * 128:(j + 1) * 128], stg_k[:, j, :], ident_f),
                         reads=['stg_k', 'cst_f'], writes=[bkk])
                P.op('act', lambda e, bk=bk: e.activation(out=kz0[0:64, 0:512], in_=bk[0:64, 0:512], func=AF.Copy), reads=[bkk], writes=['kaT'])
                P.op('act', lambda e, bk=bk: e.activation(out=kz1[64:128, 0:512], in_=bk[64:128, 0:512], func=AF.Copy), reads=[bkk], writes=['kaT'])
                for jc in range(2):
                    bk, bkk = nbank()
                    for j in range(4):
                        P.op('pe', lambda e, bk=bk, j=j, jc=jc: e.transpose(
                            bk[:, j * 128:(j + 1) * 128], stg_c[:, j, jc * 128:(jc + 1) * 128], ident_f),
                            reads=['stg_c', 'cst_f'], writes=[bkk])
                    P.op('act', lambda e, bk=bk, jc=jc: e.activation(out=cnb_c[:, jc, :], in_=bk[:, 0:512], func=AF.Copy),
                         reads=[bkk], writes=['cnb_c'])
                bk, bkk = nbank()
                for j in range(4):
                    P.op('pe', lambda e, bk=bk, j=j: e.transpose(bk[0:96, j * 128:(j + 1) * 128], stg_r[:, j, :], ident_f),
                         reads=['stg_r', 'cst_f'], writes=[bkk])
                P.op('act', lambda e, bk=bk: e.activation(out=krr_c[64:96, 0:512], in_=bk[64:96, 0:512], func=AF.Copy),
                     reads=[bkk], writes=['krr_c'])
                kside_from(cnb_c, 'cnb_c', krr_c, 'krr_c', 0, 512, 0)
                P.barrier()
                AR.reset(KS['mark_w'])

            for tt in range(T // 512):
                N = 512
                col0 = cache + tt * 512
                kc0 = col0 // 128
                tsl = slice(tt * 512, (tt + 1) * 512)
                hk = hkeys(tt)
                if rope:
                    cA, tkA = sc('cosA', [512], F32, 1)
                    sA, tkA2 = sc('sinA', [512], F32, 1)
                    cB, tkB = sc('cosB', [512], F32, 1)
                    sB, tkB2 = sc('sinB', [512], F32, 1)
                    P.dma('sp', cA, d_cosA[:, tsl], writes=[tkA])
                    P.dma('sp', sA, d_sinA[:, tsl], writes=[tkA])
                    P.dma('sp', cB[0:96, :], d_cosB[:, tsl], writes=[tkB])
                    P.dma('sp', sB[0:96, :], d_sinB[:, tsl], writes=[tkB])
                bk, bkk = nbank()
                for k in range(8):
                    P.op('pe', lambda e, bk=bk, k=k, tsl=tsl: e.matmul(bk[:, 0:N], wkv[:, k, 0:128], hT[:, k, tsl],
                                                                      start=(k == 0), stop=(k == 7)), reads=['wkv'] + hk, writes=[bkk])
                kn, kk = rms_heads(bk, bkk, N, 128, V('gka', l)[:, 0:1], 64.0, bones_b)
                if rope:
                    rope_apply(kn, kk, N, 0, 128, RA_b, cA, sA, tkA, None, 'kaT',
                               outs=[(0, 64, kz0[0:64, col0:col0 + N]), (64, 128, kz1[64:128, col0:col0 + N])])
                else:
                    P.op('act', lambda e, kn=kn, col0=col0: e.activation(out=kz0[0:64, col0:col0 + N], in_=kn[0:64, 0:N], func=AF.Copy),
                         reads=[kk], writes=['kaT'])
                    P.op('act', lambda e, kn=kn, col0=col0: e.activation(out=kz1[64:128, col0:col0 + N], in_=kn[64:128, 0:N], func=AF.Copy),
                         reads=[kk], writes=['kaT'])
                if outs:
                    bk2, bk2k = nbank()
                    for b in range(4):
                        P.op('pe', lambda e, bk2=bk2, b=b, kn=kn: e.transpose(bk2[:, b * 128:(b + 1) * 128], kn[:, b * 128:(b + 1) * 128], ident_f),
                             reads=[kk, 'cst_f'], writes=[bk2k])
                    ost, ostk = sc('ost', [4, 128], F32)
                    P.op('act', lambda e, bk2=bk2, ost=ost: e.activation(out=ost, in_=bk2[:, 0:512].rearrange('p (j f) -> p j f', j=4), func=AF.Copy),
                         reads=[bk2k], writes=[ostk])
                    [P.dma('sp', d_nk[s_, l].rearrange('(j p) f -> p j f', p=128), ost[:, s_ * 2:(s_ + 1) * 2, :], reads=[ostk], writes=[]) for s_ in range(2)]
                    stop_if('p3b')
                bk, bkk = nbank()
                for b in range(4):
                    for k in range(8):
                        P.op('pe', lambda e, bk=bk, b=b, k=k, tt=tt: e.matmul(
                            bk[:, b * 128:(b + 1) * 128], hT[:, k, tt * 512 + b * 128:tt * 512 + (b + 1) * 128], wkv[:, k, 128:256],
                            start=(k == 0), stop=(k == 7)), reads=['wkv'] + hk, writes=[bkk])
                P.op('act', lambda e, bk=bk, kc0=kc0: e.activation(
                    out=va4[:, kc0:kc0 + 4, :, 0:64], in_=bk[:, 0:512].rearrange('p (j g d) -> p j g d', j=4, g=2), func=AF.Copy),
                    reads=[bkk], writes=['va'])
                if outs:
                    ost, ostk = sc('ost', [4, 128], F32)
                    P.op('dve', lambda e, bk=bk, ost=ost: e.tensor_copy(out=ost, in_=bk[:, 0:512].rearrange('p (j f) -> p j f', j=4)),
                         reads=[bkk], writes=[ostk])
                    [P.dma('sp', d_nv[s_, l].rearrange('(j p) f -> p j f', p=128), ost[:, s_ * 2:(s_ + 1) * 2, :], reads=[ostk], writes=[]) for s_ in range(2)]
                    stop_if('p3c')
                sq2, sq2k = sc('sq2', [2, 512], BF16, 1)
                cn, cnk = sc('cn', [2, 512], F32, 1)
                cnb, cnbk = sc('cnb', [2, 512], BF16, 1)
                bkc = []
                for j in range(2):
                    bk, bkk = nbank()
                    bkc.append((bk, bkk))
                    for k in range(8):
                        P.op('pe', lambda e, bk=bk, k=k, j=j, tsl=tsl: e.matmul(
                            bk[:, 0:N], wckv[:, k, j * 128:(j + 1) * 128], hT[:, k, tsl], start=(k == 0), stop=(k == 7)),
                            reads=['wckv'] + hk, writes=[bkk])
                    P.op('act', lambda e, bk=bk, j=j, sq2=sq2: e.activation(out=sq2[:, j, 0:N], in_=bk[:, 0:N], func=AF.Square),
                         reads=[bkk], writes=[sq2k])
                rs, rk = rstd_from_sq(sq2, 2, N, 256.0, sq2k)
                for j in range(2):
                    bk, bkk = bkc[j]
                    P.op('dve', lambda e, bk=bk, j=j, cn=cn, rs=rs: e.scalar_tensor_tensor(
                        out=cn[:, j, 0:N], in0=bk[:, 0:N], scalar=V('gckv', l)[:, j:j + 1], in1=rs[:, 0:N], op0=ALU.mult, op1=ALU.mult),
                        reads=[bkk, rk, 'vecs'], writes=[cnk])
                P.op('act', lambda e, cn=cn, cnb=cnb: e.activation(out=cnb, in_=cn, func=AF.Copy), reads=[cnk], writes=[cnbk])
                if outs:
                    ostc, ostck = sc('ostc', [4, 256], F32, 1)
                    for half in range(2):
                        bk2, bk2k = nbank()
                        for bb in range(2):
                            b = half * 2 + bb
                            for j in range(2):
                                P.op('pe', lambda e, bk2=bk2, bb=bb, b=b, j=j, cn=cn: e.transpose(
                                    bk2[:, bb * 256 + j * 128:bb * 256 + (j + 1) * 128], cn[:, j, b * 128:(b + 1) * 128], ident_f),
                                    reads=[cnk, 'cst_f'], writes=[bk2k])
                        P.op('dve', lambda e, bk2=bk2, half=half, ostc=ostc: e.tensor_copy(
                            out=ostc[:, half * 2:half * 2 + 2, :], in_=bk2[:, 0:512].rearrange('p (b f) -> p b f', b=2)),
                            reads=[bk2k], writes=[ostck])
                    [P.dma('sp', d_nckv[s_, l].rearrange('(j p) f -> p j f', p=128), ostc[:, s_ * 2:(s_ + 1) * 2, :], reads=[ostck], writes=[]) for s_ in range(2)]
                    stop_if('p3d')
                bk, bkk = nbank()
                for k in range(8):
                    P.op('pe', lambda e, bk=bk, k=k, tsl=tsl: e.matmul(bk[0:96, 0:N], wkr[:, k, :], hT[:, k, tsl],
                                                                      start=(k == 0), stop=(k == 7)), reads=['wkr'] + hk, writes=[bkk])
                krf, krfk = sc('krf', [512], F32, 1)
                P.op('act', lambda e, bk=bk, krf=krf: e.activation(out=krf[0:96, 0:N], in_=bk[0:96, 0:N], func=AF.Copy),
                     reads=[bkk], writes=[krfk])
                if rope:
                    krr, krrk = sc('krr', [512], F32, 1)
                    rope_apply(krf, krfk, N, 64, 96, RB_b, cB, sB, tkB, krr[64:96, 0:N], krrk)
                else:
                    krr, krrk = krf, krfk
                if outs:
                    bk2, bk2k = nbank()
                    for b in range(4):
                        for k in range(8):
                            P.op('pe', lambda e, bk2=bk2, b=b, k=k, tt=tt: e.matmul(
                                bk2[:, b * 32:(b + 1) * 32], hT[:, k, tt * 512 + b * 128:tt * 512 + (b + 1) * 128], wkr[:, k, 64:96],
                                start=(k == 0), stop=(k == 7)), reads=['wkr'] + hk, writes=[bk2k])
                    ostr, ostrk = sc('ostr', [4, 32], F32, 1)
                    P.op('dve', lambda e, bk2=bk2, ostr=ostr: e.tensor_copy(out=ostr, in_=bk2[:, 0:128].rearrange('p (b f) -> p b f', b=4)),
                         reads=[bk2k], writes=[ostrk])
                    [P.dma('sp', d_nkr[s_, l].rearrange('(j p) f -> p j f', p=128), ostr[:, s_ * 2:(s_ + 1) * 2, :], reads=[ostrk], writes=[]) for s_ in range(2)]
                    stop_if('p3e')
                kside_from(cnb, cnbk, krr, krrk, col0, N, kc0)

        rot4 = dict(i=0)

        LO_BANKS = [0, 1, 2, 3, 6, 7]

        def nbank_lo():
            i = rot4['i']
            rot4['i'] = (i + 1) % len(LO_BANKS)
            return banks[LO_BANKS[i]], 'bank%d' % LO_BANKS[i]

        def phase4a(g, l):
            gi, T, nseq, L, cache, rope = g['gi'], g['T'], g['nseq'], g['L'], g['cache'], g['rope']
            kz0, kz1, va, kbT, vb = KS['kz0'], KS['kz1'], KS['va'], KS['kbT'], KS['vb']
            NKC = (cache + T) // 128
            (wqa, wqak), (wqb, wqbk) = KS['wq']
            NQ = 512 if nseq == 1 else L
            NQB = NQ // 128
            for qt in range(T // NQ):
                q0 = qt * NQ
                tsl = slice(q0, q0 + NQ)
                hk = hkeys(q0 // 512)
                kcs = list(range(NKC)) if nseq == 1 else [qt * 2, qt * 2 + 1]
                if rope:
                    cA, tkA = sc('cosA', [512], F32, 1)
                    sA, _ = sc('sinA', [512], F32, 1)
                    cB, tkB = sc('cosB', [512], F32, 1)
                    sB, _ = sc('sinB', [512], F32, 1)
                    P.dma('sp', cA, d_cosA[:, tsl], writes=[tkA])
                    P.dma('sp', sA, d_sinA[:, tsl], writes=[tkA])
                    P.dma('sp', cB[0:96, :], d_cosB[:, tsl], writes=[tkB])
                    P.dma('sp', sB[0:96, :], d_sinB[:, tsl], writes=[tkB])
                qaT, qak = sc('qaT', [4, 512], BF16, 1)
                qbT, qbk = sc('qbT', [8, 512], BF16, 1)
                st['lo'] = False

                def proj_qa(c):
                    bk, bkk = nbank()
                    for k in range(8):
                        P.op('pe', lambda e, k=k, tsl=tsl: e.matmul(
                            bk[:, 0:NQ], wqa[:, k, c * 128:(c + 1) * 128], hT[:, k, tsl], start=(k == 0), stop=(k == 7)),
                            reads=[wqak] + hk, writes=[bkk])
                    return bk, bkk

                def post_qa(c, bk, bkk):
                    kn, kk = rms_heads(bk, bkk, NQ, 128, V('gqa', l)[:, 0:1], 64.0, bones_b)
                    if rope:
                        rope_apply(kn, kk, NQ, 0, 128, RA_b, cA, sA, tkA, qaT[:, c, 0:NQ], qak)
                    else:
                        P.op('act', lambda e: e.activation(out=qaT[:, c, 0:NQ], in_=kn[:, 0:NQ], func=AF.Copy),
                             reads=[kk], writes=[qak])

                def proj_qb(h):
                    bk, bkk = nbank()
                    for k in range(8):
                        P.op('pe', lambda e, k=k, tsl=tsl: e.matmul(
                            bk[0:96, 0:NQ], wqb[:, k, h * 96:(h + 1) * 96], hT[:, k, tsl], start=(k == 0), stop=(k == 7)),
                            reads=[wqbk] + hk, writes=[bkk])
                    return bk, bkk

                def post_qb(h, bk, bkk):
                    if rope:
                        qf, qfk = sc('qf', [512], F32, 1)
                        P.op('act', lambda e: e.activation(out=qf[0:96, 0:NQ], in_=bk[0:96, 0:NQ], func=AF.Copy),
                             reads=[bkk], writes=[qfk])
                        P.op('dve', lambda e: e.tensor_copy(out=qbT[0:64, h, 0:NQ], in_=bk[0:64, 0:NQ]),
                             reads=[bkk], writes=[qbk])
                        rope_apply(qf, qfk, NQ, 64, 96, RB_b, cB, sB, tkB, qbT[64:96, h, 0:NQ], qbk)
                    else:
                        P.op('act', lambda e: e.activation(out=qbT[0:96, h, 0:NQ], in_=bk[0:96, 0:NQ], func=AF.Copy),
                             reads=[bkk], writes=[qbk])

                jobs = [('a', c) for c in range(4)] + [('b', h) for h in range(8)]
                pendp = {}
                for ji in range(len(jobs) + 1):
                    if ji < len(jobs):
                        kind, idx = jobs[ji]
                        pendp[ji] = (proj_qa if kind == 'a' else proj_qb)(idx)
                    if ji >= 1:
                        kind, idx = jobs[ji - 1]
                        (post_qa if kind == 'a' else post_qb)(idx, *pendp.pop(ji - 1))
                st['lo'] = True
                for br in range(2):
                    otm, otmk = sc('otm', [4, 512], F32, 1)
                    LOOK = 4
                    steps = [(h, i, kc) for h in range(8) for i, kc in enumerate(kcs)]
                    nlast = len(kcs) - 1

                    def emit_qk(h, i, kc, br=br):
                        ksl = slice(kc * 128, (kc + 1) * 128)
                        if br == 0:
                            c, j = h % 4, h // 4
                            rhs = qaT[:, c, 0:NQ]
                            lhs = (kz0 if j == 0 else kz1)[:, ksl]
                            vv = va[:, kc, j * 66:j * 66 + 65]
                            scale = 0.125
                            qkey, kkey, vkey = qak, 'kaT', 'va'
                        else:
                            rhs = qbT[0:96, h, 0:NQ]
                            lhs = kbT[0:96, h, ksl]
                            vv = vb[:, kc, h * 66:h * 66 + 65]
                            scale = 96.0 ** -0.5
                            qkey, kkey, vkey = qbk, 'kbT', 'vb'
                        sbk, sbkk = nbank_lo()
                        P.op('pe', lambda e: e.matmul(sbk[:, 0:NQ], lhs, rhs, start=True, stop=True),
                             reads=[qkey, kkey], writes=[sbkk])
                        pt, ptk = sc('pt', [512], BF16, 6)
                        P.op('act', lambda e: e.activation(out=pt[:, 0:NQ], in_=sbk[:, 0:NQ], func=AF.Exp, scale=scale),
                             reads=[sbkk], writes=[ptk])
                        return pt, ptk, vv, vkey

                    def emit_pv(h, i, kc, pt, ptk, vv, vkey, otm=otm, otmk=otmk):
                        ob = 4 + (h % 2)
                        for qb in range(NQB):
                            P.op('pe', lambda e, qb=qb: e.matmul(
                                banks[ob][:, qb * 128:qb * 128 + 65], pt[:, qb * 128:(qb + 1) * 128], vv,
                                start=(i == 0 and qb == 0), stop=(i == nlast), skip_group_check=True),
                                reads=[ptk, vkey], writes=['bank%d' % ob])
                        if i == nlast:
                            obk_ = ['bank%d' % ob]
                            ov = banks[ob].rearrange('p (q c) -> p q c', c=128)
                            rd, rdk = sc('rd', [4], F32, 4)
                            P.op('dve', lambda e, rd=rd: e.reciprocal(out=rd[:, 0:NQB], in_=ov[:, 0:NQB, 64]),
                                 reads=obk_, writes=[rdk])
                            P.op('dve', lambda e, rd=rd: e.tensor_tensor(
                                out=otm[:, 0:NQB, h * 64:(h + 1) * 64], in0=ov[:, 0:NQB, 0:64],
                                in1=rd[:, 0:NQB].unsqueeze(2).broadcast_to([128, NQB, 64]), op=ALU.mult),
                                reads=obk_ + [rdk], writes=[otmk])

                    pend = {}
                    for s_i in range(len(steps) + LOOK):
                        if s_i < len(steps):
                            pend[s_i] = emit_qk(*steps[s_i])
                        t_i = s_i - LOOK
                        if t_i >= 0:
                            emit_pv(*steps[t_i], *pend.pop(t_i))
                    oT, oTk = sc('oT', [4, 512], BF16, 1)
                    for qb in range(NQB):
                        bk, bkk = nbank_lo()
                        for c in range(4):
                            P.op('pe', lambda e, bk=bk, c=c, qb=qb, otm=otm: e.transpose(
                                bk[:, c * 128:(c + 1) * 128], otm[:, qb, c * 128:(c + 1) * 128], ident_f),
                                reads=[otmk, 'cst_f'], writes=[bkk])
                        if qb % 2 == 0:
                            P.op('act', lambda e, bk=bk, qb=qb, oT=oT: e.activation(
                                out=oT[:, :, qb * 128:(qb + 1) * 128], in_=bk[:, 0:512].rearrange('p (c t) -> p c t', c=4), func=AF.Copy),
                                reads=[bkk], writes=[oTk])
                        else:
                            P.op('dve', lambda e, bk=bk, qb=qb, oT=oT: e.tensor_copy(
                                out=oT[:, :, qb * 128:(qb + 1) * 128], in_=bk[:, 0:512].rearrange('p (c t) -> p c t', c=4)),
                                reads=[bkk], writes=[oTk])
                    P.dma('sp', (s_oa if br == 0 else s_ob)[gi][:, :, tsl], oT[:, :, 0:NQ], reads=[oTk], writes=[])

            KS['pre_mg'] = wload(d_wmg[l][0], 40, 128, 'mg%d_%d' % (l, 0))

        def phase4b(g, l, xbuf):
            gi, T = g['gi'], g['T']
            AR.reset()
            N = 512
            def loads4b(tt):
                tsl = slice(tt * 512, (tt + 1) * 512)
                xt, xk = sc('xt', [8, 512], F32, 2)
                P.dma('sp', xt, xbuf[gi][:, :, tsl], writes=[xk])
                oat, oak = sc('oat', [4, 512], BF16, 2)
                obt, obk = sc('obt', [4, 512], BF16, 2)
                oct, ock = sc('oct', [8, 512], BF16, 2)
                P.dma('sp', oat, s_oa[gi][:, :, tsl], writes=[oak])
                P.dma('sp', obt, s_ob[gi][:, :, tsl], writes=[obk])
                P.dma('sp', oct, s_oc[gi][:, :, tsl], writes=[ock])
                return (xt, xk, oat, oak, obt, obk, oct, ock)
            ntile4 = T // 512
            pre4 = {0: loads4b(0)}
            for tt in range(ntile4):
                tsl = slice(tt * 512, (tt + 1) * 512)
                hk = hkeys(tt)
                if tt + 1 < ntile4:
                    pre4[tt + 1] = loads4b(tt + 1)
                xt, xk, oat, oak, obt, obk, oct, ock = pre4.pop(tt)
                merged, mk = sc('merged', [8, 512], BF16, 1)
                for m in range(8):
                    if m == 0 and KS.get('pre_mg') is not None:
                        w, wk = KS.pop('pre_mg')
                    else:
                        w, wk = wload(d_wmg[l][m], 40, 128, 'mg%d_%d' % (l, m))
                    gts = []
                    for kk_ in range(3):
                        bk, bkk = nbank()
                        for k in range(8):
                            P.op('pe', lambda e, bk=bk, k=k, kk_=kk_, w=w, tsl=tsl: e.matmul(
                                bk[:, 0:N], w[:, 16 + kk_ * 8 + k, :], hT[:, k, tsl], start=(k == 0), stop=(k == 7)),
                                reads=[wk] + hk, writes=[bkk])
                        gt, gtk = sc('gt%d' % kk_, [512], F32, 2)
                        P.op('act', lambda e, bk=bk, gt=gt: e.activation(out=gt, in_=bk[:, 0:N], func=AF.Sigmoid), reads=[bkk], writes=[gtk])
                        gts.append((gt, gtk))
                    brs = []
                    for (src, srck, nk, woff) in ((oat, oak, 4, 0), (obt, obk, 4, 4), (oct, ock, 8, 8)):
                        bk, bkk = nbank()
                        for k in range(nk):
                            P.op('pe', lambda e, bk=bk, k=k, w=w, src=src, woff=woff, nk=nk: e.matmul(
                                bk[:, 0:N], w[:, woff + k, :], src[:, k, :], start=(k == 0), stop=(k == nk - 1)),
                                reads=[wk, srck], writes=[bkk])
                        brs.append((bk, bkk))
                    t1, t1k = sc('mt1', [512], F32, 2)
                    t2, t2k = sc('mt2', [512], F32, 2)
                    P.op('dve', lambda e, t1=t1, a=gts[0][0], b=brs[0][0]: e.tensor_tensor(out=t1, in0=b[:, 0:N], in1=a, op=ALU.mult),
                         reads=[gts[0][1], brs[0][1]], writes=[t1k])
                    P.op('dve', lambda e, t2=t2, a=gts[1][0], b=brs[1][0]: e.tensor_tensor(out=t2, in0=b[:, 0:N], in1=a, op=ALU.mult),
                         reads=[gts[1][1], brs[1][1]], writes=[t2k])
                    P.op('dve', lambda e, t1=t1, t2=t2: e.tensor_tensor(out=t1, in0=t1, in1=t2, op=ALU.add), reads=[t1k, t2k], writes=[t1k])
                    P.op('dve', lambda e, t2=t2, a=gts[2][0], b=brs[2][0]: e.tensor_tensor(out=t2, in0=b[:, 0:N], in1=a, op=ALU.mult),
                         reads=[gts[2][1], brs[2][1], t1k], writes=[t2k])
                    P.op('dve', lambda e, t1=t1, t2=t2, m=m, merged=merged: e.tensor_tensor(out=merged[:, m, :], in0=t1, in1=t2, op=ALU.add),
                         reads=[t1k, t2k], writes=[mk])
                outf, ofk = sc('outf', [8, 512], F32, 1)
                sqo, sqok = sc('sq8', [8, 512], BF16, 1)
                for half in range(2):
                    w, wk = wload(d_wout[l][half], 8, 512, 'wo%d_%d' % (l, half))
                    for mm in range(4):
                        mp = half * 4 + mm
                        bk, bkk = nbank()
                        for m in range(8):
                            P.op('pe', lambda e, bk=bk, m=m, mm=mm, w=w, merged=merged: e.matmul(
                                bk[:, 0:N], w[:, m, mm * 128:(mm + 1) * 128], merged[:, m, :], start=(m == 0), stop=(m == 7)),
                                reads=[wk, mk], writes=[bkk])
                        P.op('act', lambda e, bk=bk, mp=mp, outf=outf: e.activation(out=outf[:, mp, :], in_=bk[:, 0:N], func=AF.Copy),
                             reads=[bkk], writes=[ofk])
                        P.op('act', lambda e, bk=bk, mp=mp, sqo=sqo: e.activation(out=sqo[:, mp, :], in_=bk[:, 0:N], func=AF.Square),
                             reads=[bkk], writes=[sqok])
                epilogue(outf, ofk, sqo, sqok, xt, xk, N, 0, l, gi, 1)
                P.dma('sp', xbuf[gi][:, :, tsl], xt, reads=[xk], writes=[])

        def phase5(g, l, xsrc, xdst):
            gi, T, nseq, L = g['gi'], g['T'], g['nseq'], g['L']
            AR.reset()
            tiles = []
            bnds = {}
            if nseq * L <= 512 and nseq > 1:
                tiles.append((0, nseq * L, True, True))
                bnds[(0, nseq * L)] = [k_ * L for k_ in range(1, nseq)]
            for s in (range(nseq) if not tiles else []):
                b0 = s * L
                if L <= 512:
                    tiles.append((b0, b0 + L, True, True))
                else:
                    p = 0
                    while p < L:
                        e_ = min(L, p + (511 if p == 0 else 510))
                        tiles.append((b0 + p, b0 + e_, p == 0, e_ == L))
                        p = e_
            groups = []
            for t in tiles:
                if groups and (t[1] - t[0]) <= 16 and not t[2]:
                    groups[-1].append(t)
                else:
                    groups.append([t])
            cfw = V('cfw', l)
            cfb = V('cfb', l)

            def mkctx(t, gidx, sub):
                s_, e_, st_, en_ = t
                c = dict(s=s_, e=e_, st=st_, en=en_)
                c['bnd'] = bnds.get((s_, e_), [])
                c['lo'] = s_ - (0 if st_ else 1)
                c['hi'] = e_ + (0 if en_ else 1)
                c['N'] = c['hi'] - c['lo']
                c['n'] = e_ - s_
                c['off'] = s_ - c['lo']
                c['W'] = 512 if sub == 0 else 16
                c['tag'] = 'b' if sub == 0 else 's'
                c['par'] = gidx % 2
                assert c['N'] <= c['W']
                return c

            def prologue(c):
                W, tg = c['W'], c['tag']
                c['xt'], c['xk'] = sc('xt' + tg, [8, W], F32, 2)
                P.dma('sp', c['xt'][:, :, 0:c['N']], xsrc[gi][:, :, c['lo']:c['hi']], writes=[c['xk']])
                c['h2T'], c['h2k'] = sc('h2T' + tg, [8, W], BF16, 2)
                h2T, N = c['h2T'], c['N']
                sq, ks = sc('sq8' + tg, [8, W], BF16, 1)
                xt, xk = c['xt'], c['xk']
                P.op('act', lambda e: e.activation(out=sq[:, :, 0:N], in_=xt[:, :, 0:N], func=AF.Square), reads=[xk], writes=[ks])
                rs, rk = rstd_from_sq(sq, 8, N, D, ks)
                P.op('dve', lambda e: e.tensor_tensor(out=xt[:, :, 0:N], in0=xt[:, :, 0:N],
                                                      in1=rs[:, 0:N].unsqueeze(1).broadcast_to([128, 8, N]), op=ALU.mult),
                     reads=[xk, rk], writes=[xk])
                for cc in range(8):
                    P.op('act', lambda e, cc=cc: e.activation(out=h2T[:, cc, 0:N], in_=xt[:, cc, 0:N], func=AF.Identity,
                                                             scale=modv[:, l, gi, 2, cc:cc + 1],
                                                             bias=modraw[:, l, 24 + cc, gi:gi + 1]),
                         reads=[xk, 'modv%d' % l, 'modraw%d' % l], writes=[c['h2k']])
                c['actT'], c['ak'] = sc('actT' + tg, [22, W], BF16, 1)

            def conv(c, bk, bkk, jj):
                n, off, st_, en_, W = c['n'], c['off'], c['st'], c['en'], c['W']
                acc, acck = sc('cacc' + c['tag'], [W], F32, 4)
                P.op('act', lambda e: e.activation(out=acc[:, 0:n], in_=bk[:, off:off + n], func=AF.Identity,
                                                   scale=cfw[:, 44 + jj:44 + jj + 1], bias=cfb[:, jj:jj + 1]),
                     reads=[bkk, 'vecs'], writes=[acck])
                a = 1 if st_ else 0
                cz = 1 if en_ else 0

                def ranges(lo_, hi_, excl):
                    out_, cur = [], lo_
                    for x_ in sorted(excl):
                        if lo_ <= x_ < hi_:
                            if x_ > cur:
                                out_.append((cur, x_))
                            cur = x_ + 1
                    if hi_ > cur:
                        out_.append((cur, hi_))
                    return out_
                for (r0_, r1_) in ranges(a, n, c['bnd']):
                    P.op('dve', lambda e, r0_=r0_, r1_=r1_: e.scalar_tensor_tensor(
                        out=acc[:, r0_:r1_], in0=bk[:, off - 1 + r0_:off - 1 + r1_], scalar=cfw[:, jj:jj + 1], in1=acc[:, r0_:r1_],
                        op0=ALU.mult, op1=ALU.add), reads=[bkk, acck, 'vecs'], writes=[acck])
                for (r0_, r1_) in ranges(0, n - cz, [b_ - 1 for b_ in c['bnd']]):
                    P.op('dve', lambda e, r0_=r0_, r1_=r1_: e.scalar_tensor_tensor(
                        out=acc[:, r0_:r1_], in0=bk[:, off + 1 + r0_:off + 1 + r1_], scalar=cfw[:, 88 + jj:88 + jj + 1], in1=acc[:, r0_:r1_],
                        op0=ALU.mult, op1=ALU.add), reads=[bkk, acck, 'vecs'], writes=[acck])
                return acc, acck

            def up_phase(cs, after_first=None):
                j0 = 0
                while j0 < 22:
                    JJ = min(3, 22 - j0)
                    w, wk = wload(d_wup[l][j0:j0 + JJ].rearrange('j p k m -> p j k m'), JJ * 8, 256, 'up%d_%d' % (l, j0))
                    w = w.rearrange('p (j k) m -> p j k m', j=JJ)
                    for jj in range(JJ):
                        j = j0 + jj
                        for c in cs:
                            N, n, h2T, actT = c['N'], c['n'], c['h2T'], c['actT']
                            pair = []
                            for vg in range(2):
                                bk, bkk = nbank()
                                for k in range(8):
                                    P.op('pe', lambda e, bk=bk, k=k, jj=jj, vg=vg, w=w, h2T=h2T, N=N: e.matmul(
                                        bk[:, 0:N], w[:, jj, k, vg * 128:(vg + 1) * 128], h2T[:, k, 0:N], start=(k == 0), stop=(k == 7)),
                                        reads=[wk, c['h2k']], writes=[bkk])
                                pair.append(conv(c, bk, bkk, j + 22 * vg))
                            (vf, vfk), (gf, gfk) = pair
                            P.op('act', lambda e, gf=gf, n=n: e.activation(out=gf[:, 0:n], in_=gf[:, 0:n], func=AF.Gelu_apprx_tanh),
                                 reads=[gfk], writes=[gfk])
                            P.op('dve', lambda e, gf=gf, vf=vf, j=j, n=n, actT=actT: e.tensor_tensor(
                                out=actT[:, j, 0:n], in0=gf[:, 0:n], in1=vf[:, 0:n], op=ALU.mult),
                                reads=[gfk, vfk], writes=[c['ak']])
                    j0 += JJ
                    if after_first is not None and j0 >= 6:
                        after_first()
                        after_first = None

            def down_phase(cs):
                for c in cs:
                    c['outf'], c['ofk'] = sc('outf' + c['tag'], [8, c['W']], F32, 1)
                    c['sqo'], c['sqok'] = sc('sqo' + c['tag'], [8, c['W']], BF16, 1)
                for pc in range(4):
                    w, wk = wload(d_wdn[l][pc], 22, 256, 'dn%d_%d' % (l, pc))
                    for mm in range(2):
                        mp = pc * 2 + mm
                        for c in cs:
                            n, actT, outf, sqo = c['n'], c['actT'], c['outf'], c['sqo']
                            bk, bkk = nbank()
                            for j in range(22):
                                P.op('pe', lambda e, bk=bk, j=j, mm=mm, w=w, actT=actT, n=n: e.matmul(
                                    bk[:, 0:n], w[:, j, mm * 128:(mm + 1) * 128], actT[:, j, 0:n], start=(j == 0), stop=(j == 21)),
                                    reads=[wk, c['ak']], writes=[bkk])
                            P.op('act', lambda e, bk=bk, mp=mp, outf=outf, n=n: e.activation(out=outf[:, mp, 0:n], in_=bk[:, 0:n], func=AF.Copy),
                                 reads=[bkk], writes=[c['ofk']])
                            P.op('act', lambda e, bk=bk, mp=mp, sqo=sqo, n=n: e.activation(out=sqo[:, mp, 0:n], in_=bk[:, 0:n], func=AF.Square),
                                 reads=[bkk], writes=[c['sqok']])

            def epi(c):
                n = c['n']
                xr, xrk = c['xt'], c['xk']
                P.dma('sp', xr[:, :, 0:n], xsrc[gi][:, :, c['s']:c['e']], writes=[xrk])
                epilogue(c['outf'], c['ofk'], c['sqo'], c['sqok'], xr, xrk, n, 0, l, gi, 2)
                if l < DEPTH - 1:
                    P.dma('sp', xdst[gi][:, :, c['s']:c['e']], xr[:, :, 0:n], reads=[xrk], writes=[])
                    if g['cache']:
                        sq_, sqk_, of2, ofk2 = c['sqo'], c['sqok'], c['outf'], c['ofk']
                        P.op('act', lambda e: e.activation(out=sq_[:, :, 0:n], in_=xr[:, :, 0:n], func=AF.Square), reads=[xrk], writes=[sqk_])
                        rs2, rk2 = rstd_from_sq(sq_, 8, n, D, sqk_, 1)
                        P.op('dve', lambda e: e.tensor_tensor(out=of2[:, :, 0:n], in0=xr[:, :, 0:n],
                                                              in1=rs2[:, 0:n].unsqueeze(1).broadcast_to([128, 8, n]), op=ALU.mult),
                             reads=[xrk, rk2], writes=[ofk2])
                        for cc in range(8):
                            P.op('act', lambda e, cc=cc: e.activation(out=hT[:, cc, c['s']:c['e']], in_=of2[:, cc, 0:n], func=AF.Identity,
                                                                     scale=modv[:, l + 1, gi, 0, cc:cc + 1],
                                                                     bias=modraw[:, l + 1, cc, gi:gi + 1]),
                                 reads=[ofk2, 'modv%d' % (l + 1), 'modraw%d' % (l + 1)], writes=['hTnext'])
                    return
                for t0 in range(0, n, 128):
                    m_ = min(128, n - t0)
                    yo, yok = sc('yo', [1024], F32, 2)
                    for half in range(2):
                        bk, bkk = nbank()
                        for cc in range(4):
                            P.op('pe', lambda e, bk=bk, cc=cc, half=half, t0=t0, m_=m_: e.transpose(
                                bk[0:m_, cc * 128:(cc + 1) * 128], xr[:, half * 4 + cc, t0:t0 + m_], ident_f),
                                reads=[xrk, 'cst_f'], writes=[bkk])
                        if half == 0:
                            P.op('act', lambda e, bk=bk, yo=yo, m_=m_: e.activation(out=yo[0:m_, 0:512], in_=bk[0:m_, 0:512], func=AF.Copy),
                                 reads=[bkk], writes=[yok + 'a'])
                        else:
                            P.op('dve', lambda e, bk=bk, yo=yo, m_=m_: e.tensor_copy(out=yo[0:m_, 512:1024], in_=bk[0:m_, 0:512]),
                                 reads=[bkk], writes=[yok + 'b'])
                    r0_ = c['s'] + t0
                    P.dma('sp', d_y[gi][r0_:r0_ + m_, :], yo[0:m_, :], reads=[yok + 'a', yok + 'b'], writes=[])

            gctx = [[mkctx(t, gi_, si) for si, t in enumerate(grp)] for gi_, grp in enumerate(groups)]
            for c in gctx[0]:
                prologue(c)
            pend_epi = []
            for gi_ in range(len(gctx)):
                cs = gctx[gi_]

                def flush(pe_=pend_epi):
                    for c in pe_:
                        epi(c)
                    del pe_[:]
                up_phase(cs, flush if pend_epi else None)
                if gi_ + 1 < len(gctx):
                    for c in gctx[gi_ + 1]:
                        prologue(c)
                down_phase(cs)
                pend_epi.extend(cs)
            for c in pend_epi:
                epi(c)

        stop_if('x0')
        for l in range(DEPTH):
            if l == 1:
                run_deferred(len(DEFER))
            xcur, xnxt = (xA, xB) if l == 0 else (xB, xA)
            for g in (GR if l == 0 else [GR[1], GR[0]]):
                tag = 'l%dg%d' % (l, g['gi'])
                if not (l > 0 and g['cache']):
                    phase1(g, l, xcur)
                P.barrier()
                stop_if(tag + 'p1')
                phase2(g, l)
                P.barrier()
                stop_if(tag + 'p2')
                phase3(g, l)
                P.barrier()
                stop_if(tag + 'p3')
                AR.reset(KS['mark'])
                st['lo'] = True
                phase4a(g, l)
                st['lo'] = False
                P.barrier()
                stop_if(tag + 'p4a')
                phase4b(g, l, xcur)
                P.barrier()
                stop_if(tag + 'p4b')
                phase5(g, l, xcur, xnxt)
                P.barrier()
                stop_if(tag + 'p5')

    except _Stop:
        pass
    import collections
    print('ops per engine', collections.Counter(o.eng for o in P.ops))
    P.emit()
    _PROG['P'] = P
    print('signals', {e: max([o.sig or 0 for o in P.ops if o.eng == e] + [0]) for e in ENGS}, 'dmas', P.n_dma)
    return nc


def _fm(v):
    v = np.asarray(v, np.float32)
    lead = v.shape[:-1]
    n = v.shape[-1] // 128
    v = v.reshape(lead + (n, 128))
    return np.moveaxis(v, -1, 0)


def _rope_consts():
    f32 = np.float32
    consts = np.zeros((128, 5, 128), f32)
    consts[:, 0, :] = 1.0
    consts[0:64, 1, 0:64] = 1.0
    consts[64:128, 1, 64:128] = 1.0
    consts[:, 2, :] = np.eye(128, dtype=f32)

    def rmat(d):
        n = d // 4
        R = np.zeros((d, d), f32)
        for i in range(n):
            R[i, n + i] = -1.0
            R[n + i, i] = 1.0
            R[2 * n + i, 3 * n + i] = -1.0
            R[3 * n + i, 2 * n + i] = 1.0
        return R
    RA = rmat(64)
    consts[0:64, 3, 0:64] = RA.T
    consts[64:128, 3, 64:128] = RA.T
    consts[64:96, 4, 64:96] = rmat(32).T

    def tables(d, T=T_S):
        n = d // 4
        t = np.arange(T)
        row = (t // 64).astype(f32)
        col = (t % 64).astype(f32)
        inv = (f32(10000.0) ** (-np.arange(n, dtype=f32) / f32(n))).astype(f32)
        ar = (row[:, None] * inv[None, :]).astype(f32)
        ac = (col[:, None] * inv[None, :]).astype(f32)
        ang = np.concatenate([ar, ar, ac, ac], axis=1)
        return np.cos(ang).astype(f32).T.copy(), np.sin(ang).astype(f32).T.copy()
    cA, sA = tables(64)
    cB, sB = tables(32)
    cosA = np.concatenate([cA, cA], 0)
    sinA = np.concatenate([sA, sA], 0)
    cosB = np.zeros((96, T_S), f32)
    sinB = np.zeros((96, T_S), f32)
    cosB[64:96] = cB
    sinB[64:96] = sB
    return consts, cosA, sinA, cosB, sinB


_PROG = {}


def kernel(x_prompt, x_sample, c, cache_gqa_k, cache_gqa_v, cache_mla_ckv, cache_mla_krope,
           state_rglru_fwd, state_rglru_bwd, c_ctx, w_ada, b_ada, g_pre_mix, g_post_mix,
           g_pre_ffn, g_post_ffn, w_in, g_qa, g_ka, g_ckv, w_uk, w_uv, conv_rnn_w, conv_rnn_b,
           w_rg, b_rg, w_ig, b_ig, lam, w_oa, w_ob, w_oc, w_out, w_up, conv_ffn_w, conv_ffn_b, w_down):
    f32 = np.float32
    A = lambda a: np.ascontiguousarray(np.asarray(a, f32))
    NC = 8
    consts, cosA, sinA, cosB, sinB = _rope_consts()
    w_in = np.asarray(w_in, f32)
    w_up = np.asarray(w_up, f32)

    def kp(w, kc):
        w = np.asarray(w, f32)
        return A(w.reshape(w.shape[0], kc, 128, w.shape[2]).transpose(0, 2, 1, 3))
    def pc_(w, m):
        L_, p_, kc_, M_ = w.shape
        return A(w.reshape(L_, p_, kc_, M_ // m, m).transpose(0, 3, 1, 2, 4))
    perm = [0, 4, 1, 5, 2, 6, 3, 7]
    wqa = w_in[:, :, 0:512].reshape(DEPTH, D, 8, 64)[:, :, perm, :].reshape(DEPTH, D, 512)
    wkr = np.zeros((DEPTH, D, 96), f32)
    wkr[:, :, 64:96] = w_in[:, :, O_KR:O_KR + 32]
    wmg = np.concatenate([np.asarray(w_oa, f32), np.asarray(w_ob, f32), np.asarray(w_oc, f32),
                          w_in[:, :, O_GL:O_GL + 1024].copy(), w_in[:, :, O_GL + 1024:O_GL + 2048].copy(),
                          w_in[:, :, O_GL + 2048:O_GL + 3072].copy()], axis=1)
    wup = w_up.reshape(DEPTH, D, 2, 22, 128).transpose(0, 3, 1, 2, 4).reshape(DEPTH, 22, 8, 128, 256).transpose(0, 1, 3, 2, 4)
    shared = {
        'consts': consts, 'cosA': cosA, 'sinA': sinA, 'cosB': cosB, 'sinB': sinB,
        'wada': pc_(kp(w_ada, 8), 512), 'win': kp(w_in, 8), 'wqa': kp(wqa, 8), 'wkr': kp(wkr, 8),
        'wmg': pc_(kp(wmg, 40), 128), 'wout': pc_(kp(w_out, 8), 512), 'wup': A(wup), 'wdn': pc_(kp(w_down, 22), 256),
        'wuk': kp(w_uk, 2), 'wuv': kp(w_uv, 2),
        'wrg': A(np.asarray(w_rg, f32).transpose(0, 3, 1, 2, 4)), 'wig': A(np.asarray(w_ig, f32).transpose(0, 3, 1, 2, 4)),
    }
    vbase = np.zeros((128, NV), f32)

    def put(vv, name, arr):
        a, k = VOFF[name]
        vv[:, a:a + k] = np.asarray(arr, f32).reshape(128, k)
    for l in range(DEPTH):
        put(vbase, 'gpm%d' % l, _fm(g_pre_mix[l]))
        put(vbase, 'gqm%d' % l, _fm(g_post_mix[l]))
        put(vbase, 'gpf%d' % l, _fm(g_pre_ffn[l]))
        put(vbase, 'gqf%d' % l, _fm(g_post_ffn[l]))
        put(vbase, 'bada%d' % l, _fm(b_ada[l]))
        put(vbase, 'gqa%d' % l, np.tile(np.asarray(g_qa[l], f32), 2)[:, None])
        put(vbase, 'gka%d' % l, np.tile(np.asarray(g_ka[l], f32), 2)[:, None])
        put(vbase, 'gckv%d' % l, _fm(g_ckv[l]))
        put(vbase, 'crw%d' % l, _fm(conv_rnn_w[l]))
        put(vbase, 'crb%d' % l, _fm(conv_rnn_b[l]))
        put(vbase, 'brg%d' % l, _fm(b_rg[l]))
        put(vbase, 'big%d' % l, _fm(b_ig[l]))
        put(vbase, 'lam%d' % l, _fm(lam[l]))
        put(vbase, 'cfw%d' % l, _fm(conv_ffn_w[l]))
        put(vbase, 'cfb%d' % l, _fm(conv_ffn_b[l]))
    xp = np.asarray(x_prompt, f32)
    xs = np.asarray(x_sample, f32)
    in_maps = []
    for core in range(NC):
        b = core % 4
        vv = vbase.copy()
        cond = np.stack([np.asarray(c_ctx, f32), np.asarray(c, f32)[b]], 0)
        put(vv, 'cond', np.moveaxis(_fm(cond), 1, 2))
        stt = np.stack([np.asarray(state_rglru_fwd, f32)[b], np.asarray(state_rglru_bwd, f32)[b]], 1)
        put(vv, 'st', _fm(stt))
        m = dict(shared)
        m.update({
            'xp': A(xp[2 * core:2 * core + 2].reshape(T_P, D)), 'xs': A(xs[b]), 'vecs': vv,
            'ck': A(np.asarray(cache_gqa_k, f32)[b].reshape(DEPTH, PAST, 128)),
            'cv': A(np.asarray(cache_gqa_v, f32)[b].reshape(DEPTH, PAST, 128)),
            'cckv': A(np.asarray(cache_mla_ckv, f32)[b]), 'ckr': A(np.asarray(cache_mla_krope, f32)[b]),
        })
        in_maps.append(m)
    if 'nc' not in _PROG:
        _PROG['nc'] = build_program()
    res = run_bass_kernel_spmd(_PROG['nc'], in_maps, core_ids=list(range(NC)))
    R = res.results
    _PROG['last'] = R
    y_p = np.concatenate([R[i]['y_p'].reshape(2, L_P, D) for i in range(NC)], 0)
    y_s = np.stack([R[i]['y_s'] for i in range(4)], 0)
    nk = np.concatenate([R[i]['nk'].reshape(2, DEPTH, L_P, 2, 64) for i in range(NC)], 0)
    nv = np.concatenate([R[i]['nv'].reshape(2, DEPTH, L_P, 2, 64) for i in range(NC)], 0)
    nckv = np.concatenate([R[i]['nckv'] for i in range(NC)], 0)
    nkr = np.concatenate([R[i]['nkr'] for i in range(NC)], 0)
    nf = np.concatenate([R[i]['nf'] for i in range(NC)], 0)
    nb = np.concatenate([R[i]['nb'] for i in range(NC)], 0)
    return tuple(np.ascontiguousarray(a.astype(np.float32)) for a in (y_p, y_s, nk, nv, nckv, nkr, nf, nb))
```

```python
import contextlib
import numpy as np
import concourse.bass as bass
import concourse.mybir as mybir
from concourse.bass_utils import run_bass_kernel_spmd

F32 = mybir.dt.float32
BF16 = mybir.dt.bfloat16
AF = mybir.ActivationFunctionType
ALU = mybir.AluOpType

ENGS = ['pe', 'act', 'dve', 'pool', 'sp']
NDSEM = 12


class _Op:
    __slots__ = ('eng', 'fn', 'deps', 'is_dma', 'sig', 'dsem', 'dprev')

    def __init__(self, eng, fn, deps, is_dma):
        self.eng = eng
        self.fn = fn
        self.deps = deps
        self.is_dma = is_dma
        self.sig = None
        self.dsem = None
        self.dprev = None


class Prog:
    def __init__(self, nc):
        self.nc = nc
        self.ops = []
        self.lastw = {}
        self.rds = {}
        self.pending = {e: set() for e in ENGS}
        self.last_on = {}
        self.dmas_since_bar = []
        self.n_dma = {e: 0 for e in ENGS}
        self.marks = []

    def sb(self, name, shape, dtype):
        return self.nc.alloc_sbuf_tensor('sb_' + name, list(shape), dtype)

    def ps(self, name, shape, dtype=F32):
        return self.nc.alloc_psum_tensor('ps_' + name, list(shape), dtype)

    def _deps(self, eng, reads, writes, is_dma):
        deps = {}
        for k in reads:
            w = self.lastw.get(k)
            if w is not None:
                deps[w] = 'raw'
        for k in writes:
            w = self.lastw.get(k)
            if w is not None and w not in deps:
                deps[w] = 'waw'
            for r in self.rds.get(k, ()):
                if r not in deps:
                    deps[r] = 'war'
        out = []
        for d, kind in deps.items():
            o = self.ops[d]
            if (not o.is_dma) and o.eng == eng and kind != 'raw' and not is_dma:
                continue
            out.append(d)
        for d in self.pending[eng]:
            if d not in deps:
                out.append(d)
        self.pending[eng] = set()
        return out

    def _record(self, idx, reads, writes):
        o = self.ops[idx]
        for k in reads:
            lst = self.rds.setdefault(k, [])
            if not o.is_dma:
                for i, r in enumerate(lst):
                    if (not self.ops[r].is_dma) and self.ops[r].eng == o.eng:
                        lst[i] = idx
                        break
                else:
                    lst.append(idx)
            else:
                lst.append(idx)
        for k in writes:
            self.lastw[k] = idx
            self.rds[k] = []

    def op(self, eng, fn, reads=(), writes=()):
        ex = [k for k in reads if isinstance(k, str) and k.startswith('bank')]
        if ex:
            reads = [k for k in reads if k not in ex]
            writes = list(writes) + ex
        deps = self._deps(eng, reads, writes, False)
        idx = len(self.ops)
        self.ops.append(_Op(eng, fn, deps, False))
        self._record(idx, reads, writes)
        self.last_on[eng] = idx
        return idx

    def dma(self, q, out, in_, reads=(), writes=(), **kw):
        deps = self._deps(q, reads, writes, True)
        idx = len(self.ops)
        o = _Op(q, (out, in_, kw), deps, True)
        j = self.n_dma[q]
        self.n_dma[q] += 1
        o.dsem = (q, j % NDSEM, 16 * (j // NDSEM + 1))
        if j >= NDSEM:
            o.dprev = (q, j % NDSEM, 16 * (j // NDSEM))
        self.ops.append(o)
        self._record(idx, reads, writes)
        self.last_on[q] = idx
        self.dmas_since_bar.append(idx)
        return idx

    def barrier(self):
        s = set(self.last_on.values()) | set(self.dmas_since_bar)
        for e in ENGS:
            self.pending[e] |= s
        self.dmas_since_bar = []

    def emit(self):
        nc = self.nc
        self.barrier()
        self.op('sp', None)
        needed = set()
        for o in self.ops:
            for d in o.deps:
                if not self.ops[d].is_dma:
                    needed.add(d)
        cnt = {e: 0 for e in ENGS}
        for i, o in enumerate(self.ops):
            if i in needed:
                cnt[o.eng] += 1
                o.sig = cnt[o.eng]
        per_eng = {e: [o for o in self.ops if o.eng == e] for e in ENGS}
        with contextlib.ExitStack() as st:
            esem = {e: st.enter_context(nc.semaphore("sg_" + e)) for e in ENGS}
            dsem = {}
            for q in ENGS:
                if self.n_dma[q]:
                    for j in range(min(NDSEM, self.n_dma[q])):
                        dsem[(q, j)] = st.enter_context(nc.semaphore("sd_%s_%d" % (q, j)))
            block = st.enter_context(nc.Block())
            ops = self.ops

            def run(engname):
                def body(eng):
                    waited = {}
                    for o in per_eng[engname]:
                        want = {}
                        for d in o.deps:
                            p = ops[d]
                            if p.is_dma:
                                key = ('d', p.dsem[0], p.dsem[1])
                                val = p.dsem[2]
                            else:
                                key = ('e', p.eng)
                                val = p.sig
                            if val > want.get(key, 0):
                                want[key] = val
                        if o.dprev is not None:
                            key = ('d', o.dprev[0], o.dprev[1])
                            if o.dprev[2] > want.get(key, 0):
                                want[key] = o.dprev[2]
                        for key, val in want.items():
                            if val > waited.get(key, 0):
                                waited[key] = val
                                sem = esem[key[1]] if key[0] == 'e' else dsem[(key[1], key[2])]
                                eng.wait_ge(sem, val)
                        if o.fn is None:
                            continue
                        if o.is_dma:
                            out, in_, kw = o.fn
                            ins = eng.dma_start(out=out, in_=in_, **kw)
                            ins.then_inc(dsem[(o.dsem[0], o.dsem[1])], 16)
                        else:
                            ins = o.fn(eng)
                            if o.sig is not None:
                                ins.then_inc(esem[engname], 1)
                return body

            if per_eng['pe']:
                block.tensor(run('pe'))
            if per_eng['act']:
                block.scalar(run('act'))
            if per_eng['dve']:
                block.vector(run('dve'))
            if per_eng['pool']:
                block.gpsimd(run('pool'))
            if per_eng['sp']:
                block.sync(run('sp'))


D = 1024
DEPTH = 2
T_P, L_P = 512, 256
T_S = 2048
PAST = 512
D_FF = 2816
EPS = 1e-6
O_QA, O_KA, O_VA, O_QB, O_CKV, O_KR, O_XR, O_YR, O_GL = 0, 512, 640, 768, 1536, 1792, 1824, 2848, 3872
IN_W = 6944


def vec_layout():
    off = {}
    n = 0
    for l in range(DEPTH):
        for nm, k in [('gpm', 8), ('gqm', 8), ('gpf', 8), ('gqf', 8), ('bada', 48), ('gqa', 1), ('gka', 1),
                      ('gckv', 2), ('crw', 32), ('crb', 8), ('brg', 16), ('big', 16), ('lam', 16),
                      ('cfw', 132), ('cfb', 44)]:
            off['%s%d' % (nm, l)] = (n, k)
            n += k
    off['cond'] = (n, 16)
    n += 16
    off['st'] = (n, 32)
    n += 32
    return off, n


VOFF, NV = vec_layout()


class Arena:
    def __init__(self, P, nbytes):
        self.t = P.sb('arena', [128, nbytes // 4], F32)
        self.cap = nbytes
        self.off = 0
        self.gen = 0

    def reset(self, to=0):
        self.off = to
        self.gen += 1

    def alloc(self, shape, dt):
        n = 1
        for s in shape:
            n *= s
        b = n * (2 if dt == BF16 else 4)
        b32 = (b + 63) // 64 * 64
        a = self.off
        self.off += b32
        assert self.off <= self.cap, ("arena overflow", self.off, self.cap)
        v = self.t[:, a // 4:(a + b32) // 4]
        if dt != F32:
            v = v.bitcast(dt)
        v = v[:, 0:n]
        if len(shape) > 1:
            names = ['d%d' % i for i in range(len(shape))]
            kw = {names[i]: shape[i] for i in range(1, len(shape))}
            v = v.rearrange('p (%s) -> p %s' % (' '.join(names), ' '.join(names)), **kw)
        return v


def build_program():
    import os
    STOP = os.environ.get('MK_STOP', '')
    DBG = bool(os.environ.get('MK_DBG', ''))
    nc = bass.Bass("TRN2", target_bir_lowering=False)
    P = Prog(nc)

    def din(name, shape):
        return nc.dram_tensor(name, list(shape), F32, kind="ExternalInput").ap()

    def dout(name, shape):
        return nc.dram_tensor(name, list(shape), F32, kind="ExternalOutput").ap()

    def dscr(name, shape, dt):
        return nc.dram_tensor(name, list(shape), dt, kind=("ExternalOutput" if DBG else "Internal")).ap()

    class _Stop(Exception):
        pass

    def stop_if(tag):
        P.marks.append((tag, len(P.ops)))
        print('arena', tag, AR.off)
        if STOP == tag:
            raise _Stop()

    d_x = [din('xp', [T_P, D]), din('xs', [T_S, D])]
    d_vecs = din('vecs', [128, NV])
    d_consts = din('consts', [128, 5, 128])
    d_cosA = din('cosA', [128, T_S])
    d_sinA = din('sinA', [128, T_S])
    d_cosB = din('cosB', [96, T_S])
    d_sinB = din('sinB', [96, T_S])
    d_ck = din('ck', [DEPTH, PAST, 128])
    d_cv = din('cv', [DEPTH, PAST, 128])
    d_cckv = din('cckv', [DEPTH, PAST, 256])
    d_ckr = din('ckr', [DEPTH, PAST, 32])
    d_wada = din('wada', [DEPTH, 12, 128, 8, 512])
    d_win = din('win', [DEPTH, 128, 8, IN_W])
    d_wqa = din('wqa', [DEPTH, 128, 8, 512])
    d_wkr = din('wkr', [DEPTH, 128, 8, 96])
    d_wmg = din('wmg', [DEPTH, 8, 128, 40, 128])
    d_wout = din('wout', [DEPTH, 2, 128, 8, 512])
    d_wup = din('wup', [DEPTH, 22, 128, 8, 256])
    d_wdn = din('wdn', [DEPTH, 4, 128, 22, 256])
    d_wuk = din('wuk', [DEPTH, 128, 2, 512])
    d_wuv = din('wuv', [DEPTH, 128, 2, 512])
    d_wrg = din('wrg', [DEPTH, 128, 2, 8, 128])
    d_wig = din('wig', [DEPTH, 128, 2, 8, 128])

    d_y = [dout('y_p', [T_P, D]), dout('y_s', [T_S, D])]
    d_nk = dout('nk', [2, DEPTH, L_P, 128])
    d_nv = dout('nv', [2, DEPTH, L_P, 128])
    d_nckv = dout('nckv', [2, DEPTH, L_P, 256])
    d_nkr = dout('nkr', [2, DEPTH, L_P, 32])
    d_nf = dout('nf', [2, DEPTH, D])
    d_nb = dout('nb', [2, DEPTH, D])

    GR = [dict(gi=0, T=T_P, nseq=2, L=L_P, cache=0, rope=False, outs=True),
          dict(gi=1, T=T_S, nseq=1, L=T_S, cache=PAST, rope=True, outs=False)]
    xA = [dscr('xA%d' % g['gi'], [128, 8, g['T']], F32) for g in GR]
    xB = [dscr('xB%d' % g['gi'], [128, 8, g['T']], F32) for g in GR]
    s_oc = [dscr('soc%d' % g['gi'], [128, 8, g['T']], BF16) for g in GR]
    s_oa = [dscr('soa%d' % g['gi'], [128, 4, g['T']], BF16) for g in GR]
    s_ob = [dscr('sob%d' % g['gi'], [128, 4, g['T']], BF16) for g in GR]

    cst_f = P.sb('cst_f', [128, 5, 128], F32)
    cst_b = P.sb('cst_b', [128, 5, 128], BF16)
    vecs = P.sb('vecs', [128, NV], F32)
    modraw = P.sb('modraw', [128, DEPTH, 48, 2], F32)
    modv = P.sb('modv', [128, DEPTH, 2, 4, 8], F32)
    clv = P.sb('clv', [128, DEPTH, 16], F32)
    hT = P.sb('hT', [128, 8, T_S], BF16)
    WB = 12288
    wbufs = [P.sb('wbuf%d' % i, [128, WB // 2], BF16) for i in range(3)]
    PS = P.ps('psall', [128, 8 * 512], F32)
    PS3 = PS[:, :].rearrange('p (b n) -> p b n', b=8)
    banks = [PS3[:, i, :] for i in range(8)]
    AR = Arena(P, 130 * 1024)
    print("sbuf remaining after static", nc.sbuf_bytes_remaining)

    ones_b = cst_b[:, 0, :]
    bones_b = cst_b[:, 1, :]
    ident_f = cst_f[:, 2, :]
    RA_b = cst_b[:, 3, :]
    RB_b = cst_b[0:96, 4, 0:96]

    st = dict(bank=0, wb=0, uid=0)

    def nbank():
        if st.get('lo'):
            return nbank_lo()
        i = st['bank']
        st['bank'] = (i + 1) % 8
        return banks[i], 'bank%d' % i

    def nwb():
        i = st['wb']
        st['wb'] = (i + 1) % 3
        return wbufs[i], 'wbuf%d' % i

    def uid(s):
        st['uid'] += 1
        return '%s_%d' % (s, st['uid'])

    def V(name, l=None):
        a, k = VOFF[name if l is None else '%s%d' % (name, l)]
        return vecs[:, a:a + k]

    wcache = {}

    def wload(src, kc, m, ck=None):
        wb, wk = nwb()
        assert kc * m * 2 <= WB
        v = wb[:, 0:kc * m].rearrange('p (k m) -> p k m', k=kc)
        if ck is not None and ck in wcache:
            P.dma('pool', v, wcache[ck], reads=[], writes=[wk])
            return v, wk
        P.dma('pool', v, src, reads=[], writes=[wk])
        if ck is not None:
            wcache[ck] = dscr('wc_' + ck, [128, kc, m], BF16)
            P.dma('sp', wcache[ck], v, reads=[wk], writes=[])
        return v, wk

    try:
        P.dma('sp', cst_f[:], d_consts, writes=['cst_f'])
        P.dma('pool', cst_b[:], d_consts, writes=['cst_b'])
        P.dma('sp', vecs[:], d_vecs, writes=['vecs'])
        scb = P.sb('scb', [128, 16], BF16)
        P.op('act', lambda e: e.activation(out=scb[:], in_=V('cond'), func=AF.Silu), reads=['vecs'], writes=['scb'])
        def cl_ops(l):
            tz = [P.sb(uid('tz'), [128, 16], F32) for _ in range(6)]
            lam = V('lam', l)
            kz = uid('kz')
            seq = [
                ('dve', lambda e: e.tensor_scalar(out=tz[0][:], in0=lam, scalar1=-1.0, scalar2=None, op0=ALU.mult)),
                ('dve', lambda e: e.tensor_tensor(out=tz[0][:], in0=tz[0][:], in1=lam, op=ALU.max)),
                ('act', lambda e: e.activation(out=tz[1][:], in_=tz[0][:], func=AF.Exp, scale=-1.0)),
                ('dve', lambda e: e.tensor_scalar(out=tz[2][:], in0=tz[1][:], scalar1=2.0, scalar2=None, op0=ALU.add)),
                ('dve', lambda e: e.reciprocal(out=tz[3][:], in_=tz[2][:])),
                ('dve', lambda e: e.tensor_tensor(out=tz[2][:], in0=tz[1][:], in1=tz[3][:], op=ALU.mult)),
                ('dve', lambda e: e.tensor_tensor(out=tz[3][:], in0=tz[2][:], in1=tz[2][:], op=ALU.mult)),
                ('dve', lambda e: e.tensor_scalar(out=tz[4][:], in0=tz[3][:], scalar1=1.0 / 9, scalar2=1.0 / 7, op0=ALU.mult, op1=ALU.add)),
                ('dve', lambda e: e.tensor_tensor(out=tz[5][:], in0=tz[4][:], in1=tz[3][:], op=ALU.mult)),
                ('dve', lambda e: e.tensor_scalar(out=tz[4][:], in0=tz[5][:], scalar1=1.0 / 5, scalar2=None, op0=ALU.add)),
                ('dve', lambda e: e.tensor_tensor(out=tz[5][:], in0=tz[4][:], in1=tz[3][:], op=ALU.mult)),
                ('dve', lambda e: e.tensor_scalar(out=tz[4][:], in0=tz[5][:], scalar1=1.0 / 3, scalar2=None, op0=ALU.add)),
                ('dve', lambda e: e.tensor_tensor(out=tz[5][:], in0=tz[4][:], in1=tz[3][:], op=ALU.mult)),
                ('dve', lambda e: e.tensor_scalar(out=tz[4][:], in0=tz[5][:], scalar1=1.0, scalar2=None, op0=ALU.add)),
                ('dve', lambda e: e.tensor_tensor(out=tz[5][:], in0=tz[4][:], in1=tz[2][:], op=ALU.mult)),
                ('dve', lambda e: e.tensor_scalar(out=tz[0][:], in0=lam, scalar1=-1.0, scalar2=0.0, op0=ALU.mult, op1=ALU.max)),
                ('dve', lambda e: e.scalar_tensor_tensor(out=tz[1][:], in0=tz[5][:], scalar=2.0, in1=tz[0][:], op0=ALU.mult, op1=ALU.add)),
                ('dve', lambda e: e.tensor_scalar(out=clv[:, l, :], in0=tz[1][:], scalar1=-8.0, scalar2=None, op0=ALU.mult)),
            ]
            for en, fn in seq:
                P.op(en, fn, reads=[kz, 'vecs'], writes=[kz, 'clv%d' % l])

        DEFER = []

        def ada_piece(l, pc):
            w, wk = wload(d_wada[l][pc], 8, 512)
            bk, bkk = nbank()
            for jj in range(4):
                for k in range(8):
                    P.op('pe', lambda e, jj=jj, k=k: e.matmul(
                        bk[:, jj * 2:(jj + 1) * 2], w[:, k, jj * 128:(jj + 1) * 128], scb[:, k * 2:(k + 1) * 2],
                        start=(k == 0), stop=(k == 7)), reads=[wk, 'scb'], writes=[bkk])
            P.op('dve', lambda e: e.tensor_tensor(
                out=modraw[:, l, pc * 4:(pc + 1) * 4, :], in0=bk[:, 0:8].rearrange('p (j g) -> p j g', g=2),
                in1=V('bada', l)[:, pc * 4:(pc + 1) * 4].unsqueeze(2).broadcast_to([128, 4, 2]), op=ALU.add),
                reads=[bkk, 'vecs'], writes=['modraw%d' % l])

        def ada_finish(l):
            for g in range(2):
                P.op('dve', lambda e, g=g: e.scalar_tensor_tensor(
                    out=modv[:, l, g, 0, :], in0=modraw[:, l, 8:16, g], scalar=1.0, in1=V('gpm', l),
                    op0=ALU.add, op1=ALU.mult), reads=['modraw%d' % l, 'vecs'], writes=['modv%d' % l])
                P.op('dve', lambda e, g=g: e.tensor_tensor(
                    out=modv[:, l, g, 1, :], in0=modraw[:, l, 16:24, g], in1=V('gqm', l), op=ALU.mult),
                    reads=['modraw%d' % l, 'vecs'], writes=['modv%d' % l])
                P.op('dve', lambda e, g=g: e.scalar_tensor_tensor(
                    out=modv[:, l, g, 2, :], in0=modraw[:, l, 32:40, g], scalar=1.0, in1=V('gpf', l),
                    op0=ALU.add, op1=ALU.mult), reads=['modraw%d' % l, 'vecs'], writes=['modv%d' % l])
                P.op('dve', lambda e, g=g: e.tensor_tensor(
                    out=modv[:, l, g, 3, :], in0=modraw[:, l, 40:48, g], in1=V('gqf', l), op=ALU.mult),
                    reads=['modraw%d' % l, 'vecs'], writes=['modv%d' % l])
            cl_ops(l)

        for pc in range(12):
            ada_piece(0, pc)
        ada_finish(0)
        for pc in range(12):
            DEFER.append(lambda pc=pc: ada_piece(1, pc))
        DEFER.append(lambda: ada_finish(1))

        def run_deferred(k_):
            for _ in range(k_):
                if DEFER:
                    DEFER.pop(0)()
        stop_if('p0')

        AR.reset()
        xin = [AR.alloc([D], F32) for _ in range(4)]
        xfm = [AR.alloc([8, 128], F32) for _ in range(4)]
        pools = {}

        def sc(name, shape, dt, n=2):
            key = (AR.gen, name)
            if key not in pools:
                pools[key] = [[AR.alloc(shape, dt) for _ in range(n)], 0]
            pl = pools[key]
            i = pl[1] % len(pl[0])
            pl[1] += 1
            return pl[0][i], '%s_g%d_%d' % (name, AR.gen, i)
        nblk = 0
        for g in GR:
            gi = g['gi']
            for tb in range(g['T'] // 128):
                b = nblk % 4
                nblk += 1
                P.dma('sp', xin[b], d_x[gi][tb * 128:(tb + 1) * 128, :], writes=['xin%d' % b])
                for half in range(2):
                    bk, bkk = nbank()
                    for cc in range(4):
                        c = half * 4 + cc
                        P.op('pe', lambda e, bk=bk, cc=cc, c=c, b=b: e.transpose(
                            bk[:, cc * 128:(cc + 1) * 128], xin[b][:, c * 128:(c + 1) * 128], ident_f),
                            reads=['xin%d' % b, 'cst_f'], writes=[bkk])
                    eng = 'act' if half == 0 else 'dve'
                    if eng == 'act':
                        P.op('act', lambda e, bk=bk, b=b, half=half: e.activation(
                            out=xfm[b][:, half * 4:(half + 1) * 4, :], in_=bk[:, :].rearrange('p (c t) -> p c t', c=4),
                            func=AF.Copy), reads=[bkk], writes=['xfm%d_%d' % (b, half)])
                    else:
                        P.op('dve', lambda e, bk=bk, b=b, half=half: e.tensor_copy(
                            out=xfm[b][:, half * 4:(half + 1) * 4, :], in_=bk[:, :].rearrange('p (c t) -> p c t', c=4)),
                            reads=[bkk], writes=['xfm%d_%d' % (b, half)])
                P.dma('pool', xA[gi][:, :, tb * 128:(tb + 1) * 128], xfm[b],
                      reads=['xfm%d_0' % b, 'xfm%d_1' % b], writes=[])
        P.barrier()

        def rstd_from_sq(sq, C, N, dtot, sqkeys, nb=1):
            bk, bkk = nbank()
            for c in range(C):
                P.op('pe', lambda e, bk=bk, c=c: e.matmul(bk[:, 0:N], ones_b, sq[:, c, 0:N], start=(c == 0), stop=(c == C - 1)),
                     reads=[sqkeys[c] if isinstance(sqkeys, list) else sqkeys, 'cst_b'], writes=[bkk])
            std, k1 = sc('std%d' % nb, [512], F32, nb)
            P.op('act', lambda e: e.activation(out=std[:, 0:N], in_=bk[:, 0:N], func=AF.Sqrt, bias=EPS, scale=1.0 / dtot),
                 reads=[bkk], writes=[k1])
            rs, k2 = sc('rstd%d' % nb, [512], F32, nb)
            P.op('dve', lambda e: e.reciprocal(out=rs[:, 0:N], in_=std[:, 0:N]), reads=[k1], writes=[k2])
            return rs, k2

        def norm_mod(xt, xk, N, l, gi, which, out_fn, out_keys, nb=1):
            sq, ks = sc('sq8_%d' % nb, [8, 512], BF16, nb)
            P.op('act', lambda e: e.activation(out=sq[:, :, 0:N], in_=xt[:, :, 0:N], func=AF.Square), reads=[xk], writes=[ks])
            rs, rk = rstd_from_sq(sq, 8, N, D, ks, nb)
            P.op('dve', lambda e: e.tensor_tensor(out=xt[:, :, 0:N], in0=xt[:, :, 0:N],
                                                  in1=rs[:, 0:N].unsqueeze(1).broadcast_to([128, 8, N]), op=ALU.mult),
                 reads=[xk, rk], writes=[xk])
            ai = 0 if which == 1 else 2
            sec = 0 if which == 1 else 24
            for c in range(8):
                P.op('act', lambda e, c=c: e.activation(out=out_fn(c), in_=xt[:, c, 0:N], func=AF.Identity,
                                                         scale=modv[:, l, gi, ai, c:c + 1],
                                                         bias=modraw[:, l, sec + c, gi:gi + 1]),
                     reads=[xk, 'modv%d' % l, 'modraw%d' % l], writes=[out_keys[c]])

        def epilogue(outf, ok, sq, sqk, xt, xk, N, xoff, l, gi, which):
            rs, rk = rstd_from_sq(sq, 8, N, D, sqk)
            P.op('dve', lambda e: e.tensor_tensor(out=outf[:, :, 0:N], in0=outf[:, :, 0:N],
                                                  in1=rs[:, 0:N].unsqueeze(1).broadcast_to([128, 8, N]), op=ALU.mult),
                 reads=[ok, rk], writes=[ok])
            gidx = 1 if which == 1 else 3
            for c in range(8):
                P.op('dve', lambda e, c=c: e.scalar_tensor_tensor(
                    out=xt[:, c, xoff:xoff + N], in0=outf[:, c, 0:N], scalar=modv[:, l, gi, gidx, c:c + 1],
                    in1=xt[:, c, xoff:xoff + N], op0=ALU.mult, op1=ALU.add), reads=[ok, xk, 'modv%d' % l], writes=[xk])

        def rms_heads(bk, bkk, N, rows, gvec, hd, lhs_ones):
            sq, ks = sc('sqh', [512], BF16, 1)
            P.op('act', lambda e: e.activation(out=sq[0:rows, 0:N], in_=bk[0:rows, 0:N], func=AF.Square), reads=[bkk], writes=[ks])
            b2, b2k = nbank()
            P.op('pe', lambda e: e.matmul(b2[0:rows, 0:N], lhs_ones[0:rows, 0:rows], sq[0:rows, 0:N], start=True, stop=True),
                 reads=[ks, 'cst_b'], writes=[b2k])
            std, k1 = sc('stdh', [512], F32, 1)
            P.op('act', lambda e: e.activation(out=std[0:rows, 0:N], in_=b2[0:rows, 0:N], func=AF.Sqrt, bias=EPS, scale=1.0 / hd),
                 reads=[b2k], writes=[k1])
            rs, k2 = sc('rstdh', [512], F32, 1)
            P.op('dve', lambda e: e.reciprocal(out=rs[0:rows, 0:N], in_=std[0:rows, 0:N]), reads=[k1], writes=[k2])
            kn, kk = sc('kn', [512], F32, 1)
            P.op('dve', lambda e: e.scalar_tensor_tensor(out=kn[0:rows, 0:N], in0=bk[0:rows, 0:N], scalar=gvec,
                                                         in1=rs[0:rows, 0:N], op0=ALU.mult, op1=ALU.mult),
                 reads=[bkk, k2, 'vecs'], writes=[kk])
            return kn, kk

        def rope_apply(src, sk, N, r0, r1, R_lhsT, cos_t, sin_t, tk, out_ap, out_key, outs=None):
            sb16, k16 = sc('s16', [512], BF16, 1)
            P.op('act', lambda e: e.activation(out=sb16[0:r1, 0:N], in_=src[0:r1, 0:N], func=AF.Copy), reads=[sk], writes=[k16])
            bk, bkk = nbank()
            P.op('pe', lambda e: e.matmul(bk[0:r1, 0:N], R_lhsT, sb16[0:r1, 0:N], start=True, stop=True),
                 reads=[k16, 'cst_b'], writes=[bkk])
            t1, k1 = sc('t1', [512], F32, 1)
            t2, k2 = sc('t2', [512], F32, 1)
            P.op('dve', lambda e: e.tensor_tensor(out=t1[r0:r1, 0:N], in0=src[r0:r1, 0:N], in1=cos_t[r0:r1, 0:N], op=ALU.mult),
                 reads=[sk, tk], writes=[k1])
            P.op('dve', lambda e: e.tensor_tensor(out=t2[r0:r1, 0:N], in0=bk[r0:r1, 0:N], in1=sin_t[r0:r1, 0:N], op=ALU.mult),
                 reads=[bkk, tk], writes=[k2])
            for (ra_, rb_, oap) in (outs or [(r0, r1, out_ap)]):
                P.op('dve', lambda e, ra_=ra_, rb_=rb_, oap=oap: e.tensor_tensor(out=oap, in0=t1[ra_:rb_, 0:N], in1=t2[ra_:rb_, 0:N], op=ALU.add),
                     reads=[k1, k2], writes=[out_key])

        def hkeys(tt):
            return ['hT%d_%d' % (tt, c) for c in range(8)]

        def phase1(g, l, xsrc):
            gi, T = g['gi'], g['T']
            AR.reset()
            for tt in range(T // 512):
                xt, xk = sc('xt', [8, 512], F32)
                P.dma('sp', xt, xsrc[gi][:, :, tt * 512:(tt + 1) * 512], writes=[xk])
                norm_mod(xt, xk, 512, l, gi, 1, lambda c, tt=tt: hT[:, c, tt * 512:(tt + 1) * 512], hkeys(tt), nb=2)

        def phase2(g, l):
            gi, T, nseq, L, outs = g['gi'], g['T'], g['nseq'], g['L'], g['outs']
            AR.reset()
            ntile = T // 512
            wrg = AR.alloc([2, 8, 128], BF16)
            wig = AR.alloc([2, 8, 128], BF16)
            P.dma('pool', wrg, d_wrg[l], writes=['wrg'])
            P.dma('pool', wig, d_wig[l], writes=['wig'])
            xpad2 = [AR.alloc([nseq, L + 3], F32) for _ in range(2)]
            uf2 = [AR.alloc([T], F32) for _ in range(2)]
            ubf2 = [AR.alloc([T], BF16) for _ in range(2)]
            ra2 = [AR.alloc([T], F32) for _ in range(2)]
            ib2 = [AR.alloc([T], F32) for _ in range(2)]
            tm = AR.alloc([T], F32)
            hh = [AR.alloc([T], F32) for _ in range(2)]
            gy = AR.alloc([T], F32)
            stout = AR.alloc([32], F32)
            stT = AR.alloc([128], F32)
            for pb in range(2):
                P.op('dve', lambda e, pb=pb: e.memset(xpad2[pb][:, :, 0:2], 0.0), writes=['xpad%d' % pb])
                P.op('dve', lambda e, pb=pb: e.memset(xpad2[pb][:, :, L + 2:L + 3], 0.0), writes=['xpad%d' % pb])
            crw = V('crw', l)
            ncol = slice(0, 128)

            def stageA(n):
                pb = n % 2
                xpad, uf, ubf = xpad2[pb], uf2[pb], ubf2[pb]
                xpk, ufk, ubk = 'xpad%d' % pb, 'u%d' % pb, 'ubf%d' % pb
                wxr, wxrk = sc('wxr', [8, 128], BF16, 2)
                P.dma('pool', wxr, d_win[l][:, :, O_XR + n * 128:O_XR + (n + 1) * 128], writes=[wxrk])
                for tt in range(ntile):
                    bk, bkk = nbank()
                    for k in range(8):
                        P.op('pe', lambda e, bk=bk, k=k, tt=tt: e.matmul(
                            bk[:, 0:512], wxr[:, k, ncol], hT[:, k, tt * 512:(tt + 1) * 512], start=(k == 0), stop=(k == 7)),
                            reads=[wxrk] + hkeys(tt), writes=[bkk])
                    if nseq == 1:
                        P.op('act', lambda e, bk=bk, tt=tt: e.activation(out=xpad[:, 0, 2 + tt * 512:2 + (tt + 1) * 512],
                                                                         in_=bk[:, 0:512], func=AF.Copy),
                             reads=[bkk], writes=[xpk])
                    else:
                        P.op('act', lambda e, bk=bk: e.activation(out=xpad[:, :, 2:2 + L],
                                                                  in_=bk[:, 0:512].rearrange('p (s t) -> p s t', s=nseq),
                                                                  func=AF.Copy), reads=[bkk], writes=[xpk])
                u3 = uf.rearrange('p (s t) -> p s t', s=nseq)
                P.op('dve', lambda e: e.tensor_scalar(out=u3, in0=xpad[:, :, 0:L], scalar1=crw[:, n:n + 1],
                                                      scalar2=V('crb', l)[:, n:n + 1], op0=ALU.mult, op1=ALU.add),
                     reads=[xpk, 'vecs'], writes=[ufk])
                for k in range(1, 4):
                    P.op('dve', lambda e, k=k: e.scalar_tensor_tensor(
                        out=u3, in0=xpad[:, :, k:k + L], scalar=crw[:, k * 8 + n:k * 8 + n + 1], in1=u3,
                        op0=ALU.mult, op1=ALU.add), reads=[xpk, ufk, 'vecs'], writes=[ufk])
                P.op('act', lambda e: e.activation(out=ubf, in_=uf, func=AF.Copy), reads=[ufk], writes=[ubk])

            def stageY(n):
                wyr, wyrk = sc('wyr', [8, 128], BF16, 2)
                P.dma('pool', wyr, d_win[l][:, :, O_YR + n * 128:O_YR + (n + 1) * 128], writes=[wyrk])
                for tt in range(ntile):
                    bk, bkk = nbank()
                    for k in range(8):
                        P.op('pe', lambda e, bk=bk, k=k, tt=tt: e.matmul(
                            bk[:, 0:512], wyr[:, k, ncol], hT[:, k, tt * 512:(tt + 1) * 512], start=(k == 0), stop=(k == 7)),
                            reads=[wyrk] + hkeys(tt), writes=[bkk])
                    P.op('act', lambda e, bk=bk, tt=tt: e.activation(out=gy[:, tt * 512:(tt + 1) * 512], in_=bk[:, 0:512],
                                                                     func=AF.Gelu_apprx_tanh), reads=[bkk], writes=['gy'])

            def stageB(n, d, part):
                pb = n % 2
                uf, ubf = uf2[pb], ubf2[pb]
                ufk, ubk = 'u%d' % pb, 'ubf%d' % pb
                dn = d * 8 + n
                ra, ib = ra2[d], ib2[d]
                rak, ibk = 'ra%d' % d, 'ib%d' % d
                for tt in (range(ntile) if part == 0 else []):
                    ts_ = slice(tt * 512, (tt + 1) * 512)
                    for (wg, bname, dst, dk) in ((wrg, 'brg', ra, rak), (wig, 'big', ib, ibk)):
                        bk, bkk = nbank()
                        P.op('pe', lambda e, bk=bk, wg=wg, ts_=ts_: e.matmul(
                            bk[:, 0:512], wg[:, d, n, :], ubf[:, ts_], start=True, stop=True),
                            reads=['wrg', 'wig', ubk], writes=[bkk])
                        P.op('act', lambda e, bk=bk, dst=dst, ts_=ts_, bname=bname: e.activation(
                            out=dst[:, ts_], in_=bk[:, 0:512], func=AF.Sigmoid, bias=V(bname, l)[:, dn:dn + 1]),
                            reads=[bkk, 'vecs'], writes=[dk])
                if part == 0:
                    return
                if part == 1:
                    P.op('act', lambda e: e.activation(out=ra, in_=ra, func=AF.Exp, scale=clv[:, l, dn:dn + 1]),
                         reads=[rak, 'clv%d' % l], writes=[rak])
                    return
                P.op('dve', lambda e: e.tensor_tensor(out=tm, in0=ra, in1=ra, op=ALU.mult), reads=[rak], writes=['tm'])
                P.op('act', lambda e: e.activation(out=tm, in_=tm, func=AF.Sqrt, scale=-1.0, bias=1.0), reads=['tm'], writes=['tm'])
                P.op('dve', lambda e: e.tensor_tensor(out=ib, in0=ib, in1=uf, op=ALU.mult), reads=[ibk, ufk], writes=[ibk])
                P.op('dve', lambda e: e.tensor_tensor(out=ib, in0=ib, in1=tm, op=ALU.mult), reads=[ibk, 'tm'], writes=[ibk])
                for s_ in range(nseq):
                    init = V('st')[:, (l * 2 + d) * 8 + n:(l * 2 + d) * 8 + n + 1] if g['cache'] else 0.0
                    sl = slice(s_ * L, (s_ + 1) * L)
                    if d == 0:
                        P.op('dve', lambda e, sl=sl, init=init: e.tensor_tensor_scan(
                            out=hh[0][:, sl], data0=ra[:, sl], data1=ib[:, sl], initial=init, op0=ALU.mult, op1=ALU.add),
                            reads=[rak, ibk, 'vecs'], writes=['hh0'])
                    else:
                        P.op('dve', lambda e, sl=sl, init=init: e.tensor_tensor_scan(
                            out=hh[1][:, sl][:, ::-1], data0=ra[:, sl][:, ::-1], data1=ib[:, sl][:, ::-1], initial=init,
                            op0=ALU.mult, op1=ALU.add), reads=[rak, ibk, 'vecs'], writes=['hh1'])

            def stageC(n):
                if outs:
                    for s_ in range(nseq):
                        P.op('dve', lambda e, s_=s_: e.tensor_copy(out=stout[:, (s_ * 2) * 8 + n:(s_ * 2) * 8 + n + 1],
                                                                   in_=hh[0][:, (s_ + 1) * L - 1:(s_ + 1) * L]),
                             reads=['hh0'], writes=['stout'])
                        P.op('dve', lambda e, s_=s_: e.tensor_copy(out=stout[:, (s_ * 2 + 1) * 8 + n:(s_ * 2 + 1) * 8 + n + 1],
                                                                   in_=hh[1][:, s_ * L:s_ * L + 1]),
                             reads=['hh1'], writes=['stout'])
                P.op('dve', lambda e: e.tensor_tensor(out=hh[0], in0=hh[0], in1=hh[1], op=ALU.add), reads=['hh0', 'hh1'], writes=['hh0'])
                ocb, ock = sc('ocb', [T], BF16, 1)
                P.op('dve', lambda e: e.tensor_tensor(out=ocb, in0=gy, in1=hh[0], op=ALU.mult), reads=['gy', 'hh0'], writes=[ock])
                P.dma('sp', s_oc[gi][:, n, :], ocb, reads=[ock], writes=[])

            stageA(0)
            for n in range(8):
                if n + 1 < 8:
                    stageA(n + 1)
                stageY(n)
                stageB(n, 0, 0)
                stageB(n, 1, 0)
                stageB(n, 0, 1)
                stageB(n, 1, 1)
                stageB(n, 0, 2)
                stageB(n, 1, 2)
                stageC(n)
                run_deferred(2)
            if outs:
                bk, bkk = nbank()
                P.op('pe', lambda e: e.transpose(bk[0:32, 0:128], stout, ident_f), reads=['stout', 'cst_f'], writes=[bkk])
                P.op('act', lambda e: e.activation(out=stT[0:32, :], in_=bk[0:32, 0:128], func=AF.Copy), reads=[bkk], writes=['stT'])
                for s_ in range(nseq):
                    for d in range(2):
                        dst = (d_nf if d == 0 else d_nb)[s_, l, :].rearrange('(n p) -> n p', p=128)
                        r0 = (s_ * 2 + d) * 8
                        P.dma('sp', dst, stT[r0:r0 + 8, :], reads=['stT'], writes=[])

        KS = {}

        def phase3(g, l):
            gi, T, nseq, L, cache, rope, outs = g['gi'], g['T'], g['nseq'], g['L'], g['cache'], g['rope'], g['outs']
            AR.reset()
            KS['wq'] = (wload(d_wqa[l], 8, 512), wload(d_win[l][:, :, O_QB:O_QB + 768], 8, 768))
            S_tot = cache + T
            NKC = S_tot // 128
            kz0 = AR.alloc([S_tot], BF16)
            kz1 = AR.alloc([S_tot], BF16)
            va = AR.alloc([NKC, 132], BF16)
            kbT = AR.alloc([8, S_tot], BF16)
            vb = AR.alloc([NKC, 528], BF16)
            KS.update(kz0=kz0, kz1=kz1, va=va, kbT=kbT, vb=vb, mark=AR.off)
            wkv = AR.alloc([8, 256], BF16)
            wckv = AR.alloc([8, 256], BF16)
            wkr = AR.alloc([8, 96], BF16)
            wuk = AR.alloc([2, 512], BF16)
            wuv = AR.alloc([2, 512], BF16)
            P.dma('pool', wkv, d_win[l][:, :, O_KA:O_KA + 256], writes=['wkv'])
            P.dma('pool', wckv, d_win[l][:, :, O_CKV:O_CKV + 256], writes=['wckv'])
            P.dma('pool', wkr, d_wkr[l], writes=['wkr'])
            P.dma('pool', wuk, d_wuk[l], writes=['wuk'])
            P.dma('pool', wuv, d_wuv[l], writes=['wuv'])
            P.op('dve', lambda e: e.memset(kz0, 0.0), writes=['kaT'])
            P.op('dve', lambda e: e.memset(kz1, 0.0), writes=['kaT'])
            P.op('dve', lambda e: e.memset(va, 1.0), writes=['va'])
            P.op('dve', lambda e: e.memset(vb, 1.0), writes=['vb'])
            KS['mark_w'] = AR.off
            stop_if('p3a')
            va4 = va.rearrange('p j (g e) -> p j g e', g=2)
            vb4 = vb.rearrange('p j (h e) -> p j h e', h=8)

            def kside_from(cnb, cnbk, krr, krrk, col0, N, kc0):
                for h in range(8):
                    bk, bkk = nbank()
                    for j in range(2):
                        P.op('pe', lambda e, bk=bk, j=j, h=h: e.matmul(
                            bk[0:64, 0:N], wuk[:, j, h * 64:(h + 1) * 64], cnb[:, j, 0:N], start=(j == 0), stop=(j == 1)),
                            reads=['wuk', cnbk], writes=[bkk])
                    if h % 2 == 0:
                        P.op('act', lambda e, bk=bk, h=h: e.activation(out=kbT[0:64, h, col0:col0 + N], in_=bk[0:64, 0:N], func=AF.Copy),
                             reads=[bkk], writes=['kbT'])
                    else:
                        P.op('dve', lambda e, bk=bk, h=h: e.tensor_copy(out=kbT[0:64, h, col0:col0 + N], in_=bk[0:64, 0:N]),
                             reads=[bkk], writes=['kbT'])
                P.op('dve', lambda e: e.tensor_copy(out=kbT[64:96, :, col0:col0 + N],
                                                    in_=krr[64:96, 0:N].unsqueeze(1).broadcast_to([32, 8, N])),
                     reads=[krrk], writes=['kbT'])
                for blk in range(N // 128):
                    bk, bkk = nbank()
                    for j in range(2):
                        P.op('pe', lambda e, bk=bk, j=j, blk=blk: e.matmul(
                            bk[:, 0:512], cnb[:, j, blk * 128:(blk + 1) * 128], wuv[:, j, :], start=(j == 0), stop=(j == 1)),
                            reads=['wuv', cnbk], writes=[bkk])
                    P.op('act', lambda e, bk=bk, blk=blk: e.activation(
                        out=vb4[:, kc0 + blk, :, 0:64], in_=bk[:, 0:512].rearrange('p (h d) -> p h d', h=8), func=AF.Copy),
                        reads=[bkk], writes=['vb'])

            if cache:
                stg_k = AR.alloc([4, 128], F32)
                stg_c = AR.alloc([4, 256], F32)
                stg_r = AR.alloc([4, 96], F32)
                cnb_c = AR.alloc([2, 512], BF16)
                krr_c = AR.alloc([512], F32)
                P.dma('sp', stg_k, d_ck[l].rearrange('(j p) f -> p j f', p=128), writes=['stg_k'])
                P.dma('sp', stg_c, d_cckv[l].rearrange('(j p) f -> p j f', p=128), writes=['stg_c'])
                P.op('dve', lambda e: e.memset(stg_r, 0.0), writes=['stg_r'])
                P.dma('sp', stg_r[:, :, 64:96], d_ckr[l].rearrange('(j p) f -> p j f', p=128), writes=['stg_r'])
                for gg_ in range(2):
                    P.dma('pool', va4[:, 0:4, gg_, 0:64], d_cv[l].rearrange('(j p) (g d) -> p j g d', p=128, g=2)[:, :, gg_, :], writes=['va'])
                bk, bkk = nbank()
                for j in range(4):
                    P.op('pe', lambda e, bk=bk, j=j: e.transpose(bk[:, j * 128:(j + 1) * 128], stg_k[:, j, :], ident_f),
                         reads=['stg_k', 'cst_f'], writes=[bkk])
                P.op('act', lambda e, bk=bk: e.activation(out=kz0[0:64, 0:512], in_=bk[0:64, 0:512], func=AF.Copy), reads=[bkk], writes=['kaT'])
                P.op('act', lambda e, bk=bk: e.activation(out=kz1[64:128, 0:512], in_=bk[64:128, 0:512], func=AF.Copy), reads=[bkk], writes=['kaT'])
                for jc in range(2):
                    bk, bkk = nbank()
                    for j in range(4):
                        P.op('pe', lambda e, bk=bk, j=j, jc=jc: e.transpose(
                            bk[:, j * 128:(j + 1) * 128], stg_c[:, j, jc * 128:(jc + 1) * 128], ident_f),
                            reads=['stg_c', 'cst_f'], writes=[bkk])
                    P.op('act', lambda e, bk=bk, jc=jc: e.activation(out=cnb_c[:, jc, :], in_=bk[:, 0:512], func=AF.Copy),
                         reads=[bkk], writes=['cnb_c'])
                bk, bkk = nbank()
                for j in range(4):
                    P.op('pe', lambda e, bk=bk, j=j: e.transpose(bk[0:96, j * 128:(j + 1) * 128], stg_r[:, j, :], ident_f),
                         reads=['stg_r', 'cst_f'], writes=[bkk])
                P.op('act', lambda e, bk=bk: e.activation(out=krr_c[64:96, 0:512], in_=bk[64:96, 0:512], func=AF.Copy),
                     reads=[bkk], writes=['krr_c'])
                kside_from(cnb_c, 'cnb_c', krr_c, 'krr_c', 0, 512, 0)
                P.barrier()
                AR.reset(KS['mark_w'])

            for tt in range(T // 512):
                N = 512
                col0 = cache + tt * 512
                kc0 = col0 // 128
                tsl = slice(tt * 512, (tt + 1) * 512)
                hk = hkeys(tt)
                if rope:
                    cA, tkA = sc('cosA', [512], F32, 1)
                    sA, tkA2 = sc('sinA', [512], F32, 1)
                    cB, tkB = sc('cosB', [512], F32, 1)
                    sB, tkB2 = sc('sinB', [512], F32, 1)
                    P.dma('sp', cA, d_cosA[:, tsl], writes=[tkA])
                    P.dma('sp', sA, d_sinA[:, tsl], writes=[tkA])
                    P.dma('sp', cB[0:96, :], d_cosB[:, tsl], writes=[tkB])
                    P.dma('sp', sB[0:96, :], d_sinB[:, tsl], writes=[tkB])
                bk, bkk = nbank()
                for k in range(8):
                    P.op('pe', lambda e, bk=bk, k=k, tsl=tsl: e.matmul(bk[:, 0:N], wkv[:, k, 0:128], hT[:, k, tsl],
                                                                      start=(k == 0), stop=(k == 7)), reads=['wkv'] + hk, writes=[bkk])
                kn, kk = rms_heads(bk, bkk, N, 128, V('gka', l)[:, 0:1], 64.0, bones_b)
                if rope:
                    rope_apply(kn, kk, N, 0, 128, RA_b, cA, sA, tkA, None, 'kaT',
                               outs=[(0, 64, kz0[0:64, col0:col0 + N]), (64, 128, kz1[64:128, col0:col0 + N])])
                else:
                    P.op('act', lambda e, kn=kn, col0=col0: e.activation(out=kz0[0:64, col0:col0 + N], in_=kn[0:64, 0:N], func=AF.Copy),
                         reads=[kk], writes=['kaT'])
                    P.op('act', lambda e, kn=kn, col0=col0: e.activation(out=kz1[64:128, col0:col0 + N], in_=kn[64:128, 0:N], func=AF.Copy),
                         reads=[kk], writes=['kaT'])
                if outs:
                    bk2, bk2k = nbank()
                    for b in range(4):
                        P.op('pe', lambda e, bk2=bk2, b=b, kn=kn: e.transpose(bk2[:, b * 128:(b + 1) * 128], kn[:, b * 128:(b + 1) * 128], ident_f),
                             reads=[kk, 'cst_f'], writes=[bk2k])
                    ost, ostk = sc('ost', [4, 128], F32)
                    P.op('act', lambda e, bk2=bk2, ost=ost: e.activation(out=ost, in_=bk2[:, 0:512].rearrange('p (j f) -> p j f', j=4), func=AF.Copy),
                         reads=[bk2k], writes=[ostk])
                    [P.dma('sp', d_nk[s_, l].rearrange('(j p) f -> p j f', p=128), ost[:, s_ * 2:(s_ + 1) * 2, :], reads=[ostk], writes=[]) for s_ in range(2)]
                    stop_if('p3b')
                bk, bkk = nbank()
                for b in range(4):
                    for k in range(8):
                        P.op('pe', lambda e, bk=bk, b=b, k=k, tt=tt: e.matmul(
                            bk[:, b * 128:(b + 1) * 128], hT[:, k, tt * 512 + b * 128:tt * 512 + (b + 1) * 128], wkv[:, k, 128:256],
                            start=(k == 0), stop=(k == 7)), reads=['wkv'] + hk, writes=[bkk])
                P.op('act', lambda e, bk=bk, kc0=kc0: e.activation(
                    out=va4[:, kc0:kc0 + 4, :, 0:64], in_=bk[:, 0:512].rearrange('p (j g d) -> p j g d', j=4, g=2), func=AF.Copy),
                    reads=[bkk], writes=['va'])
                if outs:
                    ost, ostk = sc('ost', [4, 128], F32)
                    P.op('dve', lambda e, bk=bk, ost=ost: e.tensor_copy(out=ost, in_=bk[:, 0:512].rearrange('p (j f) -> p j f', j=4)),
                         reads=[bkk], writes=[ostk])
                    [P.dma('sp', d_nv[s_, l].rearrange('(j p) f -> p j f', p=128), ost[:, s_ * 2:(s_ + 1) * 2, :], reads=[ostk], writes=[]) for s_ in range(2)]
                    stop_if('p3c')
                sq2, sq2k = sc('sq2', [2, 512], BF16, 1)
                cn, cnk = sc('cn', [2, 512], F32, 1)
                cnb, cnbk = sc('cnb', [2, 512], BF16, 1)
                bkc = []
                for j in range(2):
                    bk, bkk = nbank()
                    bkc.append((bk, bkk))
                    for k in range(8):
                        P.op('pe', lambda e, bk=bk, k=k, j=j, tsl=tsl: e.matmul(
                            bk[:, 0:N], wckv[:, k, j * 128:(j + 1) * 128], hT[:, k, tsl], start=(k == 0), stop=(k == 7)),
                            reads=['wckv'] + hk, writes=[bkk])
                    P.op('act', lambda e, bk=bk, j=j, sq2=sq2: e.activation(out=sq2[:, j, 0:N], in_=bk[:, 0:N], func=AF.Square),
                         reads=[bkk], writes=[sq2k])
                rs, rk = rstd_from_sq(sq2, 2, N, 256.0, sq2k)
                for j in range(2):
                    bk, bkk = bkc[j]
                    P.op('dve', lambda e, bk=bk, j=j, cn=cn, rs=rs: e.scalar_tensor_tensor(
                        out=cn[:, j, 0:N], in0=bk[:, 0:N], scalar=V('gckv', l)[:, j:j + 1], in1=rs[:, 0:N], op0=ALU.mult, op1=ALU.mult),
                        reads=[bkk, rk, 'vecs'], writes=[cnk])
                P.op('act', lambda e, cn=cn, cnb=cnb: e.activation(out=cnb, in_=cn, func=AF.Copy), reads=[cnk], writes=[cnbk])
                if outs:
                    ostc, ostck = sc('ostc', [4, 256], F32, 1)
                    for half in range(2):
                        bk2, bk2k = nbank()
                        for bb in range(2):
                            b = half * 2 + bb
                            for j in range(2):
                                P.op('pe', lambda e, bk2=bk2, bb=bb, b=b, j=j, cn=cn: e.transpose(
                                    bk2[:, bb * 256 + j * 128:bb * 256 + (j + 1) * 128], cn[:, j, b * 128:(b + 1) * 128], ident_f),
                                    reads=[cnk, 'cst_f'], writes=[bk2k])
                        P.op('dve', lambda e, bk2=bk2, half=half, ostc=ostc: e.tensor_copy(
                            out=ostc[:, half * 2:half * 2 + 2, :], in_=bk2[:, 0:512].rearrange('p (b f) -> p b f', b=2)),
                            reads=[bk2k], writes=[ostck])
                    [P.dma('sp', d_nckv[s_, l].rearrange('(j p) f -> p j f', p=128), ostc[:, s_ * 2:(s_ + 1) * 2, :], reads=[ostck], writes=[]) for s_ in range(2)]
                    stop_if('p3d')
                bk, bkk = nbank()
                for k in range(8):
                    P.op('pe', lambda e, bk=bk, k=k, tsl=tsl: e.matmul(bk[0:96, 0:N], wkr[:, k, :], hT[:, k, tsl],
                                                                      start=(k == 0), stop=(k == 7)), reads=['wkr'] + hk, writes=[bkk])
                krf, krfk = sc('krf', [512], F32, 1)
                P.op('act', lambda e, bk=bk, krf=krf: e.activation(out=krf[0:96, 0:N], in_=bk[0:96, 0:N], func=AF.Copy),
                     reads=[bkk], writes=[krfk])
                if rope:
                    krr, krrk = sc('krr', [512], F32, 1)
                    rope_apply(krf, krfk, N, 64, 96, RB_b, cB, sB, tkB, krr[64:96, 0:N], krrk)
                else:
                    krr, krrk = krf, krfk
                if outs:
                    bk2, bk2k = nbank()
                    for b in range(4):
                        for k in range(8):
                            P.op('pe', lambda e, bk2=bk2, b=b, k=k, tt=tt: e.matmul(
                                bk2[:, b * 32:(b + 1) * 32], hT[:, k, tt * 512 + b * 128:tt * 512 + (b + 1) * 128], wkr[:, k, 64:96],
                                start=(k == 0), stop=(k == 7)), reads=['wkr'] + hk, writes=[bk2k])
                    ostr, ostrk = sc('ostr', [4, 32], F32, 1)
                    P.op('dve', lambda e, bk2=bk2, ostr=ostr: e.tensor_copy(out=ostr, in_=bk2[:, 0:128].rearrange('p (b f) -> p b f', b=4)),
                         reads=[bk2k], writes=[ostrk])
                    [P.dma('sp', d_nkr[s_, l].rearrange('(j p) f -> p j f', p=128), ostr[:, s_ * 2:(s_ + 1) * 2, :], reads=[ostrk], writes=[]) for s_ in range(2)]
                    stop_if('p3e')
                kside_from(cnb, cnbk, krr, krrk, col0, N, kc0)

        rot4 = dict(i=0)

        LO_BANKS = [0, 1, 2, 3, 6, 7]

        def nbank_lo():
            i = rot4['i']
            rot4['i'] = (i + 1) % len(LO_BANKS)
            return banks[LO_BANKS[i]], 'bank%d' % LO_BANKS[i]

        def phase4a(g, l):
            gi, T, nseq, L, cache, rope = g['gi'], g['T'], g['nseq'], g['L'], g['cache'], g['rope']
            kz0, kz1, va, kbT, vb = KS['kz0'], KS['kz1'], KS['va'], KS['kbT'], KS['vb']
            NKC = (cache + T) // 128
            (wqa, wqak), (wqb, wqbk) = KS['wq']
            NQ = 512 if nseq == 1 else L
            NQB = NQ // 128
            for qt in range(T // NQ):
                q0 = qt * NQ
                tsl = slice(q0, q0 + NQ)
                hk = hkeys(q0 // 512)
                kcs = list(range(NKC)) if nseq == 1 else [qt * 2, qt * 2 + 1]
                if rope:
                    cA, tkA = sc('cosA', [512], F32, 1)
                    sA, _ = sc('sinA', [512], F32, 1)
                    cB, tkB = sc('cosB', [512], F32, 1)
                    sB, _ = sc('sinB', [512], F32, 1)
                    P.dma('sp', cA, d_cosA[:, tsl], writes=[tkA])
                    P.dma('sp', sA, d_sinA[:, tsl], writes=[tkA])
                    P.dma('sp', cB[0:96, :], d_cosB[:, tsl], writes=[tkB])
                    P.dma('sp', sB[0:96, :], d_sinB[:, tsl], writes=[tkB])
                qaT, qak = sc('qaT', [4, 512], BF16, 1)
                qbT, qbk = sc('qbT', [8, 512], BF16, 1)
                st['lo'] = False

                def proj_qa(c):
                    bk, bkk = nbank()
                    for k in range(8):
                        P.op('pe', lambda e, k=k, tsl=tsl: e.matmul(
                            bk[:, 0:NQ], wqa[:, k, c * 128:(c + 1) * 128], hT[:, k, tsl], start=(k == 0), stop=(k == 7)),
                            reads=[wqak] + hk, writes=[bkk])
                    return bk, bkk

                def post_qa(c, bk, bkk):
                    kn, kk = rms_heads(bk, bkk, NQ, 128, V('gqa', l)[:, 0:1], 64.0, bones_b)
                    if rope:
                        rope_apply(kn, kk, NQ, 0, 128, RA_b, cA, sA, tkA, qaT[:, c, 0:NQ], qak)
                    else:
                        P.op('act', lambda e: e.activation(out=qaT[:, c, 0:NQ], in_=kn[:, 0:NQ], func=AF.Copy),
                             reads=[kk], writes=[qak])

                def proj_qb(h):
                    bk, bkk = nbank()
                    for k in range(8):
                        P.op('pe', lambda e, k=k, tsl=tsl: e.matmul(
                            bk[0:96, 0:NQ], wqb[:, k, h * 96:(h + 1) * 96], hT[:, k, tsl], start=(k == 0), stop=(k == 7)),
                            reads=[wqbk] + hk, writes=[bkk])
                    return bk, bkk

                def post_qb(h, bk, bkk):
                    if rope:
                        qf, qfk = sc('qf', [512], F32, 1)
                        P.op('act', lambda e: e.activation(out=qf[0:96, 0:NQ], in_=bk[0:96, 0:NQ], func=AF.Copy),
                             reads=[bkk], writes=[qfk])
                        P.op('dve', lambda e: e.tensor_copy(out=qbT[0:64, h, 0:NQ], in_=bk[0:64, 0:NQ]),
                             reads=[bkk], writes=[qbk])
                        rope_apply(qf, qfk, NQ, 64, 96, RB_b, cB, sB, tkB, qbT[64:96, h, 0:NQ], qbk)
                    else:
                        P.op('act', lambda e: e.activation(out=qbT[0:96, h, 0:NQ], in_=bk[0:96, 0:NQ], func=AF.Copy),
                             reads=[bkk], writes=[qbk])

                jobs = [('a', c) for c in range(4)] + [('b', h) for h in range(8)]
                pendp = {}
                for ji in range(len(jobs) + 1):
                    if ji < len(jobs):
                        kind, idx = jobs[ji]
                        pendp[ji] = (proj_qa if kind == 'a' else proj_qb)(idx)
                    if ji >= 1:
                        kind, idx = jobs[ji - 1]
                        (post_qa if kind == 'a' else post_qb)(idx, *pendp.pop(ji - 1))
                st['lo'] = True
                for br in range(2):
                    otm, otmk = sc('otm', [4, 512], F32, 1)
                    LOOK = 4
                    steps = [(h, i, kc) for h in range(8) for i, kc in enumerate(kcs)]
                    nlast = len(kcs) - 1

                    def emit_qk(h, i, kc, br=br):
                        ksl = slice(kc * 128, (kc + 1) * 128)
                        if br == 0:
                            c, j = h % 4, h // 4
                            rhs = qaT[:, c, 0:NQ]
                            lhs = (kz0 if j == 0 else kz1)[:, ksl]
                            vv = va[:, kc, j * 66:j * 66 + 65]
                            scale = 0.125
                            qkey, kkey, vkey = qak, 'kaT', 'va'
                        else:
                            rhs = qbT[0:96, h, 0:NQ]
                            lhs = kbT[0:96, h, ksl]
                            vv = vb[:, kc, h * 66:h * 66 + 65]
                            scale = 96.0 ** -0.5
                            qkey, kkey, vkey = qbk, 'kbT', 'vb'
                        sbk, sbkk = nbank_lo()
                        P.op('pe', lambda e: e.matmul(sbk[:, 0:NQ], lhs, rhs, start=True, stop=True),
                             reads=[qkey, kkey], writes=[sbkk])
                        pt, ptk = sc('pt', [512], BF16, 6)
                        P.op('act', lambda e: e.activation(out=pt[:, 0:NQ], in_=sbk[:, 0:NQ], func=AF.Exp, scale=scale),
                             reads=[sbkk], writes=[ptk])
                        return pt, ptk, vv, vkey

                    def emit_pv(h, i, kc, pt, ptk, vv, vkey, otm=otm, otmk=otmk):
                        ob = 4 + (h % 2)
                        for qb in range(NQB):
                            P.op('pe', lambda e, qb=qb: e.matmul(
                                banks[ob][:, qb * 128:qb * 128 + 65], pt[:, qb * 128:(qb + 1) * 128], vv,
                                start=(i == 0 and qb == 0), stop=(i == nlast), skip_group_check=True),
                                reads=[ptk, vkey], writes=['bank%d' % ob])
                        if i == nlast:
                            obk_ = ['bank%d' % ob]
                            ov = banks[ob].rearrange('p (q c) -> p q c', c=128)
                            rd, rdk = sc('rd', [4], F32, 4)
                            P.op('dve', lambda e, rd=rd: e.reciprocal(out=rd[:, 0:NQB], in_=ov[:, 0:NQB, 64]),
                                 reads=obk_, writes=[rdk])
                            P.op('dve', lambda e, rd=rd: e.tensor_tensor(
                                out=otm[:, 0:NQB, h * 64:(h + 1) * 64], in0=ov[:, 0:NQB, 0:64],
                                in1=rd[:, 0:NQB].unsqueeze(2).broadcast_to([128, NQB, 64]), op=ALU.mult),
                                reads=obk_ + [rdk], writes=[otmk])

                    pend = {}
                    for s_i in range(len(steps) + LOOK):
                        if s_i < len(steps):
                            pend[s_i] = emit_qk(*steps[s_i])
                        t_i = s_i - LOOK
                        if t_i >= 0:
                            emit_pv(*steps[t_i], *pend.pop(t_i))
                    oT, oTk = sc('oT', [4, 512], BF16, 1)
                    for qb in range(NQB):
                        bk, bkk = nbank_lo()
                        for c in range(4):
                            P.op('pe', lambda e, bk=bk, c=c, qb=qb, otm=otm: e.transpose(
                                bk[:, c * 128:(c + 1) * 128], otm[:, qb, c * 128:(c + 1) * 128], ident_f),
                                reads=[otmk, 'cst_f'], writes=[bkk])
                        if qb % 2 == 0:
                            P.op('act', lambda e, bk=bk, qb=qb, oT=oT: e.activation(
                                out=oT[:, :, qb * 128:(qb + 1) * 128], in_=bk[:, 0:512].rearrange('p (c t) -> p c t', c=4), func=AF.Copy),
                                reads=[bkk], writes=[oTk])
                        else:
                            P.op('dve', lambda e, bk=bk, qb=qb, oT=oT: e.tensor_copy(
                                out=oT[:, :, qb * 128:(qb + 1) * 128], in_=bk[:, 0:512].rearrange('p (c t) -> p c t', c=4)),
                                reads=[bkk], writes=[oTk])
                    P.dma('sp', (s_oa if br == 0 else s_ob)[gi][:, :, tsl], oT[:, :, 0:NQ], reads=[oTk], writes=[])

            KS['pre_mg'] = wload(d_wmg[l][0], 40, 128, 'mg%d_%d' % (l, 0))

        def phase4b(g, l, xbuf):
            gi, T = g['gi'], g['T']
            AR.reset()
            N = 512
            def loads4b(tt):
                tsl = slice(tt * 512, (tt + 1) * 512)
                xt, xk = sc('xt', [8, 512], F32, 2)
                P.dma('sp', xt, xbuf[gi][:, :, tsl], writes=[xk])
                oat, oak = sc('oat', [4, 512], BF16, 2)
                obt, obk = sc('obt', [4, 512], BF16, 2)
                oct, ock = sc('oct', [8, 512], BF16, 2)
                P.dma('sp', oat, s_oa[gi][:, :, tsl], writes=[oak])
                P.dma('sp', obt, s_ob[gi][:, :, tsl], writes=[obk])
                P.dma('sp', oct, s_oc[gi][:, :, tsl], writes=[ock])
                return (xt, xk, oat, oak, obt, obk, oct, ock)
            ntile4 = T // 512
            pre4 = {0: loads4b(0)}
            for tt in range(ntile4):
                tsl = slice(tt * 512, (tt + 1) * 512)
                hk = hkeys(tt)
                if tt + 1 < ntile4:
                    pre4[tt + 1] = loads4b(tt + 1)
                xt, xk, oat, oak, obt, obk, oct, ock = pre4.pop(tt)
                merged, mk = sc('merged', [8, 512], BF16, 1)
                for m in range(8):
                    if m == 0 and KS.get('pre_mg') is not None:
                        w, wk = KS.pop('pre_mg')
                    else:
                        w, wk = wload(d_wmg[l][m], 40, 128, 'mg%d_%d' % (l, m))
                    gts = []
                    for kk_ in range(3):
                        bk, bkk = nbank()
                        for k in range(8):
                            P.op('pe', lambda e, bk=bk, k=k, kk_=kk_, w=w, tsl=tsl: e.matmul(
                                bk[:, 0:N], w[:, 16 + kk_ * 8 + k, :], hT[:, k, tsl], start=(k == 0), stop=(k == 7)),
                                reads=[wk] + hk, writes=[bkk])
                        gt, gtk = sc('gt%d' % kk_, [512], F32, 2)
                        P.op('act', lambda e, bk=bk, gt=gt: e.activation(out=gt, in_=bk[:, 0:N], func=AF.Sigmoid), reads=[bkk], writes=[gtk])
                        gts.append((gt, gtk))
                    brs = []
                    for (src, srck, nk, woff) in ((oat, oak, 4, 0), (obt, obk, 4, 4), (oct, ock, 8, 8)):
                        bk, bkk = nbank()
                        for k in range(nk):
                            P.op('pe', lambda e, bk=bk, k=k, w=w, src=src, woff=woff, nk=nk: e.matmul(
                                bk[:, 0:N], w[:, woff + k, :], src[:, k, :], start=(k == 0), stop=(k == nk - 1)),
                                reads=[wk, srck], writes=[bkk])
                        brs.append((bk, bkk))
                    t1, t1k = sc('mt1', [512], F32, 2)
                    t2, t2k = sc('mt2', [512], F32, 2)
                    P.op('dve', lambda e, t1=t1, a=gts[0][0], b=brs[0][0]: e.tensor_tensor(out=t1, in0=b[:, 0:N], in1=a, op=ALU.mult),
                         reads=[gts[0][1], brs[0][1]], writes=[t1k])
                    P.op('dve', lambda e, t2=t2, a=gts[1][0], b=brs[1][0]: e.tensor_tensor(out=t2, in0=b[:, 0:N], in1=a, op=ALU.mult),
                         reads=[gts[1][1], brs[1][1]], writes=[t2k])
                    P.op('dve', lambda e, t1=t1, t2=t2: e.tensor_tensor(out=t1, in0=t1, in1=t2, op=ALU.add), reads=[t1k, t2k], writes=[t1k])
                    P.op('dve', lambda e, t2=t2, a=gts[2][0], b=brs[2][0]: e.tensor_tensor(out=t2, in0=b[:, 0:N], in1=a, op=ALU.mult),
                         reads=[gts[2][1], brs[2][1], t1k], writes=[t2k])
                    P.op('dve', lambda e, t1=t1, t2=t2, m=m, merged=merged: e.tensor_tensor(out=merged[:, m, :], in0=t1, in1=t2, op=ALU.add),
                         reads=[t1k, t2k], writes=[mk])
                outf, ofk = sc('outf', [8, 512], F32, 1)
                sqo, sqok = sc('sq8', [8, 512], BF16, 1)
                for half in range(2):
                    w, wk = wload(d_wout[l][half], 8, 512, 'wo%d_%d' % (l, half))
                    for mm in range(4):
                        mp = half * 4 + mm
                        bk, bkk = nbank()
                        for m in range(8):
                            P.op('pe', lambda e, bk=bk, m=m, mm=mm, w=w, merged=merged: e.matmul(
                                bk[:, 0:N], w[:, m, mm * 128:(mm + 1) * 128], merged[:, m, :], start=(m == 0), stop=(m == 7)),
                                reads=[wk, mk], writes=[bkk])
                        P.op('act', lambda e, bk=bk, mp=mp, outf=outf: e.activation(out=outf[:, mp, :], in_=bk[:, 0:N], func=AF.Copy),
                             reads=[bkk], writes=[ofk])
                        P.op('act', lambda e, bk=bk, mp=mp, sqo=sqo: e.activation(out=sqo[:, mp, :], in_=bk[:, 0:N], func=AF.Square),
                             reads=[bkk], writes=[sqok])
                epilogue(outf, ofk, sqo, sqok, xt, xk, N, 0, l, gi, 1)
                P.dma('sp', xbuf[gi][:, :, tsl], xt, reads=[xk], writes=[])

            KS['pre_up'] = wload(d_wup[l][0:3].rearrange('j p k m -> p j k m'), 24, 256, 'up%d_%d' % (l, 0))

        def phase5(g, l, xsrc, xdst):
            gi, T, nseq, L = g['gi'], g['T'], g['nseq'], g['L']
            AR.reset()
            tiles = []
            bnds = {}
            if nseq * L <= 512 and nseq > 1:
                tiles.append((0, nseq * L, True, True))
                bnds[(0, nseq * L)] = [k_ * L for k_ in range(1, nseq)]
            for s in (range(nseq) if not tiles else []):
                b0 = s * L
                if L <= 512:
                    tiles.append((b0, b0 + L, True, True))
                else:
                    p = 0
                    while p < L:
                        e_ = min(L, p + (511 if p == 0 else 510))
                        tiles.append((b0 + p, b0 + e_, p == 0, e_ == L))
                        p = e_
            groups = []
            for t in tiles:
                if groups and (t[1] - t[0]) <= 16 and not t[2]:
                    groups[-1].append(t)
                else:
                    groups.append([t])
            cfw = V('cfw', l)
            cfb = V('cfb', l)

            def mkctx(t, gidx, sub):
                s_, e_, st_, en_ = t
                c = dict(s=s_, e=e_, st=st_, en=en_)
                c['bnd'] = bnds.get((s_, e_), [])
                c['lo'] = s_ - (0 if st_ else 1)
                c['hi'] = e_ + (0 if en_ else 1)
                c['N'] = c['hi'] - c['lo']
                c['n'] = e_ - s_
                c['off'] = s_ - c['lo']
                c['W'] = 512 if sub == 0 else 16
                c['tag'] = 'b' if sub == 0 else 's'
                c['par'] = gidx % 2
                assert c['N'] <= c['W']
                return c

            def prologue(c):
                W, tg = c['W'], c['tag']
                c['xt'], c['xk'] = sc('xt' + tg, [8, W], F32, 2)
                P.dma('sp', c['xt'][:, :, 0:c['N']], xsrc[gi][:, :, c['lo']:c['hi']], writes=[c['xk']])
                c['h2T'], c['h2k'] = sc('h2T' + tg, [8, W], BF16, 2)
                h2T, N = c['h2T'], c['N']
                sq, ks = sc('sq8' + tg, [8, W], BF16, 1)
                xt, xk = c['xt'], c['xk']
                P.op('act', lambda e: e.activation(out=sq[:, :, 0:N], in_=xt[:, :, 0:N], func=AF.Square), reads=[xk], writes=[ks])
                rs, rk = rstd_from_sq(sq, 8, N, D, ks)
                P.op('dve', lambda e: e.tensor_tensor(out=xt[:, :, 0:N], in0=xt[:, :, 0:N],
                                                      in1=rs[:, 0:N].unsqueeze(1).broadcast_to([128, 8, N]), op=ALU.mult),
                     reads=[xk, rk], writes=[xk])
                for cc in range(8):
                    P.op('act', lambda e, cc=cc: e.activation(out=h2T[:, cc, 0:N], in_=xt[:, cc, 0:N], func=AF.Identity,
                                                             scale=modv[:, l, gi, 2, cc:cc + 1],
                                                             bias=modraw[:, l, 24 + cc, gi:gi + 1]),
                         reads=[xk, 'modv%d' % l, 'modraw%d' % l], writes=[c['h2k']])
                c['actT'], c['ak'] = sc('actT' + tg, [22, W], BF16, 1)

            def conv(c, bk, bkk, jj):
                n, off, st_, en_, W = c['n'], c['off'], c['st'], c['en'], c['W']
                acc, acck = sc('cacc' + c['tag'], [W], F32, 4)
                P.op('act', lambda e: e.activation(out=acc[:, 0:n], in_=bk[:, off:off + n], func=AF.Identity,
                                                   scale=cfw[:, 44 + jj:44 + jj + 1], bias=cfb[:, jj:jj + 1]),
                     reads=[bkk, 'vecs'], writes=[acck])
                a = 1 if st_ else 0
                cz = 1 if en_ else 0

                def ranges(lo_, hi_, excl):
                    out_, cur = [], lo_
                    for x_ in sorted(excl):
                        if lo_ <= x_ < hi_:
                            if x_ > cur:
                                out_.append((cur, x_))
                            cur = x_ + 1
                    if hi_ > cur:
                        out_.append((cur, hi_))
                    return out_
                for (r0_, r1_) in ranges(a, n, c['bnd']):
                    P.op('dve', lambda e, r0_=r0_, r1_=r1_: e.scalar_tensor_tensor(
                        out=acc[:, r0_:r1_], in0=bk[:, off - 1 + r0_:off - 1 + r1_], scalar=cfw[:, jj:jj + 1], in1=acc[:, r0_:r1_],
                        op0=ALU.mult, op1=ALU.add), reads=[bkk, acck, 'vecs'], writes=[acck])
                for (r0_, r1_) in ranges(0, n - cz, [b_ - 1 for b_ in c['bnd']]):
                    P.op('dve', lambda e, r0_=r0_, r1_=r1_: e.scalar_tensor_tensor(
                        out=acc[:, r0_:r1_], in0=bk[:, off + 1 + r0_:off + 1 + r1_], scalar=cfw[:, 88 + jj:88 + jj + 1], in1=acc[:, r0_:r1_],
                        op0=ALU.mult, op1=ALU.add), reads=[bkk, acck, 'vecs'], writes=[acck])
                return acc, acck

            def up_phase(cs, after_first=None):
                j0 = 0
                while j0 < 22:
                    JJ = min(3, 22 - j0)
                    if j0 == 0 and KS.get('pre_up') is not None:
                        w, wk = KS.pop('pre_up')
                    else:
                        w, wk = wload(d_wup[l][j0:j0 + JJ].rearrange('j p k m -> p j k m'), JJ * 8, 256, 'up%d_%d' % (l, j0))
                    w = w.rearrange('p (j k) m -> p j k m', j=JJ)
                    for jj in range(JJ):
                        j = j0 + jj
                        for c in cs:
                            N, n, h2T, actT = c['N'], c['n'], c['h2T'], c['actT']
                            pair = []
                            for vg in range(2):
                                bk, bkk = nbank()
                                for k in range(8):
                                    P.op('pe', lambda e, bk=bk, k=k, jj=jj, vg=vg, w=w, h2T=h2T, N=N: e.matmul(
                                        bk[:, 0:N], w[:, jj, k, vg * 128:(vg + 1) * 128], h2T[:, k, 0:N], start=(k == 0), stop=(k == 7)),
                                        reads=[wk, c['h2k']], writes=[bkk])
                                pair.append(conv(c, bk, bkk, j + 22 * vg))
                            (vf, vfk), (gf, gfk) = pair
                            P.op('act', lambda e, gf=gf, n=n: e.activation(out=gf[:, 0:n], in_=gf[:, 0:n], func=AF.Gelu_apprx_tanh),
                                 reads=[gfk], writes=[gfk])
                            P.op('dve', lambda e, gf=gf, vf=vf, j=j, n=n, actT=actT: e.tensor_tensor(
                                out=actT[:, j, 0:n], in0=gf[:, 0:n], in1=vf[:, 0:n], op=ALU.mult),
                                reads=[gfk, vfk], writes=[c['ak']])
                    j0 += JJ
                    if after_first is not None and j0 >= 6:
                        after_first()
                        after_first = None

            def down_phase(cs):
                for c in cs:
                    c['outf'], c['ofk'] = sc('outf' + c['tag'], [8, c['W']], F32, 1)
                    c['sqo'], c['sqok'] = sc('sqo' + c['tag'], [8, c['W']], BF16, 1)
                for pc in range(4):
                    w, wk = wload(d_wdn[l][pc], 22, 256, 'dn%d_%d' % (l, pc))
                    for mm in range(2):
                        mp = pc * 2 + mm
                        for c in cs:
                            n, actT, outf, sqo = c['n'], c['actT'], c['outf'], c['sqo']
                            bk, bkk = nbank()
                            for j in range(22):
                                P.op('pe', lambda e, bk=bk, j=j, mm=mm, w=w, actT=actT, n=n: e.matmul(
                                    bk[:, 0:n], w[:, j, mm * 128:(mm + 1) * 128], actT[:, j, 0:n], start=(j == 0), stop=(j == 21)),
                                    reads=[wk, c['ak']], writes=[bkk])
                            P.op('act', lambda e, bk=bk, mp=mp, outf=outf, n=n: e.activation(out=outf[:, mp, 0:n], in_=bk[:, 0:n], func=AF.Copy),
                                 reads=[bkk], writes=[c['ofk']])
                            P.op('act', lambda e, bk=bk, mp=mp, sqo=sqo, n=n: e.activation(out=sqo[:, mp, 0:n], in_=bk[:, 0:n], func=AF.Square),
                                 reads=[bkk], writes=[c['sqok']])

            def epi(c):
                n = c['n']
                xr, xrk = c['xt'], c['xk']
                P.dma('sp', xr[:, :, 0:n], xsrc[gi][:, :, c['s']:c['e']], writes=[xrk])
                epilogue(c['outf'], c['ofk'], c['sqo'], c['sqok'], xr, xrk, n, 0, l, gi, 2)
                if l < DEPTH - 1:
                    P.dma('sp', xdst[gi][:, :, c['s']:c['e']], xr[:, :, 0:n], reads=[xrk], writes=[])
                    if g['cache']:
                        sq_, sqk_, of2, ofk2 = c['sqo'], c['sqok'], c['outf'], c['ofk']
                        P.op('act', lambda e: e.activation(out=sq_[:, :, 0:n], in_=xr[:, :, 0:n], func=AF.Square), reads=[xrk], writes=[sqk_])
                        rs2, rk2 = rstd_from_sq(sq_, 8, n, D, sqk_, 1)
                        P.op('dve', lambda e: e.tensor_tensor(out=of2[:, :, 0:n], in0=xr[:, :, 0:n],
                                                              in1=rs2[:, 0:n].unsqueeze(1).broadcast_to([128, 8, n]), op=ALU.mult),
                             reads=[xrk, rk2], writes=[ofk2])
                        for cc in range(8):
                            P.op('act', lambda e, cc=cc: e.activation(out=hT[:, cc, c['s']:c['e']], in_=of2[:, cc, 0:n], func=AF.Identity,
                                                                     scale=modv[:, l + 1, gi, 0, cc:cc + 1],
                                                                     bias=modraw[:, l + 1, cc, gi:gi + 1]),
                                 reads=[ofk2, 'modv%d' % (l + 1), 'modraw%d' % (l + 1)], writes=['hTnext'])
                    return
                for t0 in range(0, n, 128):
                    m_ = min(128, n - t0)
                    yo, yok = sc('yo', [1024], F32, 2)
                    for half in range(2):
                        bk, bkk = nbank()
                        for cc in range(4):
                            P.op('pe', lambda e, bk=bk, cc=cc, half=half, t0=t0, m_=m_: e.transpose(
                                bk[0:m_, cc * 128:(cc + 1) * 128], xr[:, half * 4 + cc, t0:t0 + m_], ident_f),
                                reads=[xrk, 'cst_f'], writes=[bkk])
                        if half == 0:
                            P.op('act', lambda e, bk=bk, yo=yo, m_=m_: e.activation(out=yo[0:m_, 0:512], in_=bk[0:m_, 0:512], func=AF.Copy),
                                 reads=[bkk], writes=[yok + 'a'])
                        else:
                            P.op('dve', lambda e, bk=bk, yo=yo, m_=m_: e.tensor_copy(out=yo[0:m_, 512:1024], in_=bk[0:m_, 0:512]),
                                 reads=[bkk], writes=[yok + 'b'])
                    r0_ = c['s'] + t0
                    P.dma('sp', d_y[gi][r0_:r0_ + m_, :], yo[0:m_, :], reads=[yok + 'a', yok + 'b'], writes=[])

            gctx = [[mkctx(t, gi_, si) for si, t in enumerate(grp)] for gi_, grp in enumerate(groups)]
            for c in gctx[0]:
                prologue(c)
            pend_epi = []
            for gi_ in range(len(gctx)):
                cs = gctx[gi_]

                def flush(pe_=pend_epi):
                    for c in pe_:
                        epi(c)
                    del pe_[:]
                up_phase(cs, flush if pend_epi else None)
                if gi_ + 1 < len(gctx):
                    for c in gctx[gi_ + 1]:
                        prologue(c)
                down_phase(cs)
                pend_epi.extend(cs)
            for c in pend_epi:
                epi(c)

        stop_if('x0')
        for l in range(DEPTH):
            if l == 1:
                run_deferred(len(DEFER))
            xcur, xnxt = (xA, xB) if l == 0 else (xB, xA)
            for g in (GR if l == 0 else [GR[1], GR[0]]):
                tag = 'l%dg%d' % (l, g['gi'])
                if not (l > 0 and g['cache']):
                    phase1(g, l, xcur)
                P.barrier()
                stop_if(tag + 'p1')
                phase2(g, l)
                P.barrier()
                stop_if(tag + 'p2')
                phase3(g, l)
                P.barrier()
                stop_if(tag + 'p3')
                AR.reset(KS['mark'])
                st['lo'] = True
                phase4a(g, l)
                st['lo'] = False
                P.barrier()
                stop_if(tag + 'p4a')
                phase4b(g, l, xcur)
                P.barrier()
                stop_if(tag + 'p4b')
                phase5(g, l, xcur, xnxt)
                P.barrier()
                stop_if(tag + 'p5')

    except _Stop:
        pass
    import collections
    print('ops per engine', collections.Counter(o.eng for o in P.ops))
    P.emit()
    _PROG['P'] = P
    print('signals', {e: max([o.sig or 0 for o in P.ops if o.eng == e] + [0]) for e in ENGS}, 'dmas', P.n_dma)
    return nc


def _fm(v):
    v = np.asarray(v, np.float32)
    lead = v.shape[:-1]
    n = v.shape[-1] // 128
    v = v.reshape(lead + (n, 128))
    return np.moveaxis(v, -1, 0)


def _rope_consts():
    f32 = np.float32
    consts = np.zeros((128, 5, 128), f32)
    consts[:, 0, :] = 1.0
    consts[0:64, 1, 0:64] = 1.0
    consts[64:128, 1, 64:128] = 1.0
    consts[:, 2, :] = np.eye(128, dtype=f32)

    def rmat(d):
        n = d // 4
        R = np.zeros((d, d), f32)
        for i in range(n):
            R[i, n + i] = -1.0
            R[n + i, i] = 1.0
            R[2 * n + i, 3 * n + i] = -1.0
            R[3 * n + i, 2 * n + i] = 1.0
        return R
    RA = rmat(64)
    consts[0:64, 3, 0:64] = RA.T
    consts[64:128, 3, 64:128] = RA.T
    consts[64:96, 4, 64:96] = rmat(32).T

    def tables(d, T=T_S):
        n = d // 4
        t = np.arange(T)
        row = (t // 64).astype(f32)
        col = (t % 64).astype(f32)
        inv = (f32(10000.0) ** (-np.arange(n, dtype=f32) / f32(n))).astype(f32)
        ar = (row[:, None] * inv[None, :]).astype(f32)
        ac = (col[:, None] * inv[None, :]).astype(f32)
        ang = np.concatenate([ar, ar, ac, ac], axis=1)
        return np.cos(ang).astype(f32).T.copy(), np.sin(ang).astype(f32).T.copy()
    cA, sA = tables(64)
    cB, sB = tables(32)
    cosA = np.concatenate([cA, cA], 0)
    sinA = np.concatenate([sA, sA], 0)
    cosB = np.zeros((96, T_S), f32)
    sinB = np.zeros((96, T_S), f32)
    cosB[64:96] = cB
    sinB[64:96] = sB
    return consts, cosA, sinA, cosB, sinB


_PROG = {}


def kernel(x_prompt, x_sample, c, cache_gqa_k, cache_gqa_v, cache_mla_ckv, cache_mla_krope,
           state_rglru_fwd, state_rglru_bwd, c_ctx, w_ada, b_ada, g_pre_mix, g_post_mix,
           g_pre_ffn, g_post_ffn, w_in, g_qa, g_ka, g_ckv, w_uk, w_uv, conv_rnn_w, conv_rnn_b,
           w_rg, b_rg, w_ig, b_ig, lam, w_oa, w_ob, w_oc, w_out, w_up, conv_ffn_w, conv_ffn_b, w_down):
    f32 = np.float32
    A = lambda a: np.ascontiguousarray(np.asarray(a, f32))
    NC = 8
    consts, cosA, sinA, cosB, sinB = _rope_consts()
    w_in = np.asarray(w_in, f32)
    w_up = np.asarray(w_up, f32)

    def kp(w, kc):
        w = np.asarray(w, f32)
        return A(w.reshape(w.shape[0], kc, 128, w.shape[2]).transpose(0, 2, 1, 3))
    def pc_(w, m):
        L_, p_, kc_, M_ = w.shape
        return A(w.reshape(L_, p_, kc_, M_ // m, m).transpose(0, 3, 1, 2, 4))
    perm = [0, 4, 1, 5, 2, 6, 3, 7]
    wqa = w_in[:, :, 0:512].reshape(DEPTH, D, 8, 64)[:, :, perm, :].reshape(DEPTH, D, 512)
    wkr = np.zeros((DEPTH, D, 96), f32)
    wkr[:, :, 64:96] = w_in[:, :, O_KR:O_KR + 32]
    wmg = np.concatenate([np.asarray(w_oa, f32), np.asarray(w_ob, f32), np.asarray(w_oc, f32),
                          w_in[:, :, O_GL:O_GL + 1024].copy(), w_in[:, :, O_GL + 1024:O_GL + 2048].copy(),
                          w_in[:, :, O_GL + 2048:O_GL + 3072].copy()], axis=1)
    wup = w_up.reshape(DEPTH, D, 2, 22, 128).transpose(0, 3, 1, 2, 4).reshape(DEPTH, 22, 8, 128, 256).transpose(0, 1, 3, 2, 4)
    shared = {
        'consts': consts, 'cosA': cosA, 'sinA': sinA, 'cosB': cosB, 'sinB': sinB,
        'wada': pc_(kp(w_ada, 8), 512), 'win': kp(w_in, 8), 'wqa': kp(wqa, 8), 'wkr': kp(wkr, 8),
        'wmg': pc_(kp(wmg, 40), 128), 'wout': pc_(kp(w_out, 8), 512), 'wup': A(wup), 'wdn': pc_(kp(w_down, 22), 256),
        'wuk': kp(w_uk, 2), 'wuv': kp(w_uv, 2),
        'wrg': A(np.asarray(w_rg, f32).transpose(0, 3, 1, 2, 4)), 'wig': A(np.asarray(w_ig, f32).transpose(0, 3, 1, 2, 4)),
    }
    vbase = np.zeros((128, NV), f32)

    def put(vv, name, arr):
        a, k = VOFF[name]
        vv[:, a:a + k] = np.asarray(arr, f32).reshape(128, k)
    for l in range(DEPTH):
        put(vbase, 'gpm%d' % l, _fm(g_pre_mix[l]))
        put(vbase, 'gqm%d' % l, _fm(g_post_mix[l]))
        put(vbase, 'gpf%d' % l, _fm(g_pre_ffn[l]))
        put(vbase, 'gqf%d' % l, _fm(g_post_ffn[l]))
        put(vbase, 'bada%d' % l, _fm(b_ada[l]))
        put(vbase, 'gqa%d' % l, np.tile(np.asarray(g_qa[l], f32), 2)[:, None])
        put(vbase, 'gka%d' % l, np.tile(np.asarray(g_ka[l], f32), 2)[:, None])
        put(vbase, 'gckv%d' % l, _fm(g_ckv[l]))
        put(vbase, 'crw%d' % l, _fm(conv_rnn_w[l]))
        put(vbase, 'crb%d' % l, _fm(conv_rnn_b[l]))
        put(vbase, 'brg%d' % l, _fm(b_rg[l]))
        put(vbase, 'big%d' % l, _fm(b_ig[l]))
        put(vbase, 'lam%d' % l, _fm(lam[l]))
        put(vbase, 'cfw%d' % l, _fm(conv_ffn_w[l]))
        put(vbase, 'cfb%d' % l, _fm(conv_ffn_b[l]))
    xp = np.asarray(x_prompt, f32)
    xs = np.asarray(x_sample, f32)
    in_maps = []
    for core in range(NC):
        b = core % 4
        vv = vbase.copy()
        cond = np.stack([np.asarray(c_ctx, f32), np.asarray(c, f32)[b]], 0)
        put(vv, 'cond', np.moveaxis(_fm(cond), 1, 2))
        stt = np.stack([np.asarray(state_rglru_fwd, f32)[b], np.asarray(state_rglru_bwd, f32)[b]], 1)
        put(vv, 'st', _fm(stt))
        m = dict(shared)
        m.update({
            'xp': A(xp[2 * core:2 * core + 2].reshape(T_P, D)), 'xs': A(xs[b]), 'vecs': vv,
            'ck': A(np.asarray(cache_gqa_k, f32)[b].reshape(DEPTH, PAST, 128)),
            'cv': A(np.asarray(cache_gqa_v, f32)[b].reshape(DEPTH, PAST, 128)),
            'cckv': A(np.asarray(cache_mla_ckv, f32)[b]), 'ckr': A(np.asarray(cache_mla_krope, f32)[b]),
        })
        in_maps.append(m)
    if 'nc' not in _PROG:
        _PROG['nc'] = build_program()
    res = run_bass_kernel_spmd(_PROG['nc'], in_maps, core_ids=list(range(NC)))
    R = res.results
    _PROG['last'] = R
    y_p = np.concatenate([R[i]['y_p'].reshape(2, L_P, D) for i in range(NC)], 0)
    y_s = np.stack([R[i]['y_s'] for i in range(4)], 0)
    nk = np.concatenate([R[i]['nk'].reshape(2, DEPTH, L_P, 2, 64) for i in range(NC)], 0)
    nv = np.concatenate([R[i]['nv'].reshape(2, DEPTH, L_P, 2, 64) for i in range(NC)], 0)
    nckv = np.concatenate([R[i]['nckv'] for i in range(NC)], 0)
    nkr = np.concatenate([R[i]['nkr'] for i in range(NC)], 0)
    nf = np.concatenate([R[i]['nf'] for i in range(NC)], 0)
    nb = np.concatenate([R[i]['nb'] for i in range(NC)], 0)
    return tuple(np.ascontiguousarray(a.astype(np.float32)) for a in (y_p, y_s, nk, nv, nckv, nkr, nf, nb))
```
